# Optimizing a Trainium2 kernel written in Bass

```python
import math
import jax
import jax.numpy as jnp
from jax import lax
import numpy as np

D_MODEL = 1024
BATCH = 8
SEQ = 2048
DEPTH = 4

D_MIX = 2 * D_MODEL
SSD_WIDTH = D_MIX // 2
ATTN_WIDTH = D_MIX // 4
CM_CHANNELS = D_MIX // 4
SSD_HEAD_DIM = 64
SSD_HEADS = SSD_WIDTH // SSD_HEAD_DIM
SSD_STATE = 128
SSD_GROUPS = 2
SSD_CONV = 4
SSD_CHUNK = 128
SSD_XBC = SSD_WIDTH + 2 * SSD_GROUPS * SSD_STATE
ATTN_HEAD_DIM = 64
ATTN_Q_HEADS = ATTN_WIDTH // ATTN_HEAD_DIM
ATTN_KV_HEADS = 2
WINDOW = 128
ATTN_BLOCK = WINDOW
ROPE_THETA = 10000.0
CM_CONV_WIDTH = 31
D_FF = 4 * D_MODEL
RMS_EPS = 1e-6
LN_EPS = 1e-5
IN_SIZES = (SSD_WIDTH, SSD_XBC, SSD_HEADS,
            ATTN_Q_HEADS * ATTN_HEAD_DIM, ATTN_KV_HEADS * ATTN_HEAD_DIM, ATTN_KV_HEADS * ATTN_HEAD_DIM,
            2 * CM_CHANNELS)
N_IN = sum(IN_SIZES)

kernel_name = "hybrid_ssd_swa_conformer_parallel_heads"


def _split(u, sizes):
    idx, acc = [], 0
    for s in sizes[:-1]:
        acc += s
        idx.append(acc)
    return jnp.split(u, idx, axis=-1)


def rms_norm(x, w, eps=RMS_EPS):
    xf = x.astype(jnp.float32)
    y = xf * lax.rsqrt(jnp.mean(xf * xf, axis=-1, keepdims=True) + eps)
    return (y * w.astype(jnp.float32)).astype(x.dtype)


def layer_norm(x, w, b, eps=LN_EPS):
    xf = x.astype(jnp.float32)
    mu = jnp.mean(xf, axis=-1, keepdims=True)
    var = jnp.mean(jnp.square(xf - mu), axis=-1, keepdims=True)
    y = (xf - mu) * lax.rsqrt(var + eps)
    return (y * w.astype(jnp.float32) + b.astype(jnp.float32)).astype(x.dtype)


def gated_rms_norm(y, z, w):
    g = y.astype(jnp.float32) * jax.nn.silu(z.astype(jnp.float32))
    shp = g.shape
    g = g.reshape(shp[:-1] + (SSD_GROUPS, shp[-1] // SSD_GROUPS))
    g = g * lax.rsqrt(jnp.mean(g * g, axis=-1, keepdims=True) + RMS_EPS)
    return (g.reshape(shp) * w.astype(jnp.float32)).astype(y.dtype)


def causal_depthwise_conv(u, w, b):
    k_w, ch = w.shape
    out = lax.conv_general_dilated(
        u, w[:, None, :].astype(u.dtype), window_strides=(1,), padding=[(k_w - 1, 0)],
        dimension_numbers=("NWC", "WIO", "NWC"), feature_group_count=ch)
    return out + b.astype(u.dtype)


def rope_tables(seq_len, dim):
    inv_freq = ROPE_THETA ** (-jnp.arange(0, dim, 2, dtype=jnp.float32) / dim)
    ang = jnp.arange(seq_len, dtype=jnp.float32)[:, None] * inv_freq[None, :]
    return jnp.cos(ang), jnp.sin(ang)


def apply_rope(x, cos, sin):
    xf = x.astype(jnp.float32)
    x1, x2 = jnp.split(xf, 2, axis=-1)
    c = cos[None, :, None, :]
    s = sin[None, :, None, :]
    return jnp.concatenate([x1 * c - x2 * s, x2 * c + x1 * s], axis=-1).astype(x.dtype)


def ssd_chunked(x, dt, a, bm, cm, d_skip):
    bsz, seq, nh, hp = x.shape
    q = SSD_CHUNK
    nc = seq // q
    r = nh // SSD_GROUPS
    xf = x.astype(jnp.float32).reshape(bsz, nc, q, SSD_GROUPS, r, hp)
    dtc = dt.reshape(bsz, nc, q, SSD_GROUPS, r)
    bf = bm.astype(jnp.float32).reshape(bsz, nc, q, SSD_GROUPS, SSD_STATE)
    cf = cm.astype(jnp.float32).reshape(bsz, nc, q, SSD_GROUPS, SSD_STATE)
    a_dt = jnp.moveaxis(dtc * a.reshape(SSD_GROUPS, r), 2, -1)
    a_cs = jnp.cumsum(a_dt, axis=-1)
    xdt = xf * dtc[..., None]
    causal = jnp.tril(jnp.ones((q, q), dtype=bool))
    seg = a_cs[..., :, None] - a_cs[..., None, :]
    decay_ls = jnp.exp(jnp.where(causal, seg, -jnp.inf))
    cb = jnp.einsum("bclgn,bcsgn->bcgls", cf, bf)
    y_diag = jnp.einsum("bcgrls,bcsgrp->bclgrp", cb[:, :, :, None] * decay_ls, xdt)
    decay_s = jnp.exp(a_cs[..., -1:] - a_cs)
    states = jnp.einsum("bcsgn,bcgrs,bcsgrp->bcgrpn", bf, decay_s, xdt)
    chunk_decay = jnp.exp(a_cs[..., -1])

    def step(h, inp):
        s_c, d_c = inp
        return h * d_c[..., None, None] + s_c, h

    h0 = jnp.zeros((bsz, SSD_GROUPS, r, hp, SSD_STATE), jnp.float32)
    _, prev = lax.scan(step, h0, (jnp.moveaxis(states, 1, 0), jnp.moveaxis(chunk_decay, 1, 0)))
    prev = jnp.moveaxis(prev, 0, 1)
    y_off = jnp.einsum("bclgn,bcgrpn,bcgrl->bclgrp", cf, prev, jnp.exp(a_cs))
    y = y_diag + y_off + xf * d_skip.astype(jnp.float32).reshape(SSD_GROUPS, r)[..., None]
    return y.reshape(bsz, seq, nh * hp).astype(x.dtype)


def sliding_window_gqa(q, k, v, sinks):
    bsz, seq, hq, hd = q.shape
    hkv = k.shape[2]
    r = hq // hkv
    blk = ATTN_BLOCK
    nb = seq // blk
    qb = q.astype(jnp.float32).reshape(bsz, nb, blk, hkv, r, hd)

    def with_prev(t):
        tb = t.reshape(bsz, nb, blk, hkv, hd)
        tp = jnp.pad(tb, ((0, 0), (1, 0), (0, 0), (0, 0), (0, 0)))[:, :-1]
        return jnp.concatenate([tp, tb], axis=2)

    kc = with_prev(k).astype(jnp.float32)
    vc = with_prev(v)
    s = jnp.einsum("bnqhrd,bnkhd->bnhrqk", qb, kc) * (1.0 / math.sqrt(hd))
    qi = jnp.arange(blk)[:, None]
    ki = jnp.arange(2 * blk)[None, :]
    diff = qi + blk - ki
    band = (diff >= 0) & (diff < WINDOW)
    kpos = jnp.arange(nb)[:, None, None] * blk + ki[None] - blk
    mask = band[None] & (kpos >= 0)
    s = jnp.where(mask[None, :, None, None], s, -jnp.inf)
    sink = jnp.broadcast_to(sinks.astype(jnp.float32).reshape(1, 1, hkv, r, 1, 1), s.shape[:-1] + (1,))
    p = jax.nn.softmax(jnp.concatenate([s, sink], axis=-1), axis=-1)[..., :-1]
    o = jnp.einsum("bnhrqk,bnkhd->bnqhrd", p.astype(v.dtype), vc)
    return o.reshape(bsz, seq, hq * hd)


def conformer_conv(u, dw_w, dw_b, ln_w, ln_b):
    a, g = jnp.split(u, 2, axis=-1)
    h = a * jax.nn.sigmoid(g)
    h = causal_depthwise_conv(h, dw_w, dw_b)
    return jax.nn.silu(layer_norm(h, ln_w, ln_b))


def hybrid_layer(x, cos, sin, norm_mix_w, w_in, ssd_conv_w, ssd_conv_b, ssd_dt_bias, ssd_a_log,
                 ssd_d, ssd_norm_w, q_norm_w, k_norm_w, attn_sinks, cm_dw_w, cm_dw_b,
                 cm_ln_w, cm_ln_b, w_out, norm_mlp_w, w_mlp_up, w_mlp_down):
    bsz, seq, _ = x.shape
    h = rms_norm(x, norm_mix_w)
    u = h @ w_in
    z, xbc, dt_raw, q, k, v, glu = _split(u, IN_SIZES)
    xbc = jax.nn.silu(causal_depthwise_conv(xbc, ssd_conv_w, ssd_conv_b))
    xs, bm, cm = _split(xbc, (SSD_WIDTH, SSD_GROUPS * SSD_STATE, SSD_GROUPS * SSD_STATE))
    dt = jax.nn.softplus(dt_raw.astype(jnp.float32) + ssd_dt_bias.astype(jnp.float32))
    a = -jnp.exp(ssd_a_log.astype(jnp.float32))
    y_ssd = ssd_chunked(xs.reshape(bsz, seq, SSD_HEADS, SSD_HEAD_DIM), dt, a,
                        bm.reshape(bsz, seq, SSD_GROUPS, SSD_STATE),
                        cm.reshape(bsz, seq, SSD_GROUPS, SSD_STATE), ssd_d)
    y_ssd = gated_rms_norm(y_ssd, z, ssd_norm_w)
    q = rms_norm(q.reshape(bsz, seq, ATTN_Q_HEADS, ATTN_HEAD_DIM), q_norm_w)
    k = rms_norm(k.reshape(bsz, seq, ATTN_KV_HEADS, ATTN_HEAD_DIM), k_norm_w)
    q = apply_rope(q, cos, sin)
    k = apply_rope(k, cos, sin)
    y_attn = sliding_window_gqa(q, k, v.reshape(bsz, seq, ATTN_KV_HEADS, ATTN_HEAD_DIM), attn_sinks)
    y_conv = conformer_conv(glu, cm_dw_w, cm_dw_b, cm_ln_w, cm_ln_b)
    x = x + jnp.concatenate([y_ssd, y_attn, y_conv], axis=-1) @ w_out
    hm = rms_norm(x, norm_mlp_w)
    x = x + jnp.square(jax.nn.relu(hm @ w_mlp_up)) @ w_mlp_down
    return x


def setup_inputs(seed: int = 0) -> dict:
    key = jax.random.key(seed)
    ks = jax.random.split(key, 24)
    f32 = jnp.float32
    nrm = lambda k, shp, scale: jax.random.normal(k, shp, f32) * scale
    dt_init = jnp.exp(jax.random.uniform(ks[5], (DEPTH, SSD_HEADS), f32)
                      * (math.log(0.1) - math.log(0.001)) + math.log(0.001))
    return {
        "x": nrm(ks[0], (BATCH, SEQ, D_MODEL), 1.0),
        "norm_mix_w": 1.0 + nrm(ks[1], (DEPTH, D_MODEL), 0.02),
        "w_in": nrm(ks[2], (DEPTH, D_MODEL, N_IN), D_MODEL ** -0.5),
        "ssd_conv_w": nrm(ks[3], (DEPTH, SSD_CONV, SSD_XBC), SSD_CONV ** -0.5),
        "ssd_conv_b": nrm(ks[4], (DEPTH, SSD_XBC), 0.02),
        "ssd_dt_bias": dt_init + jnp.log(-jnp.expm1(-dt_init)),
        "ssd_a_log": jnp.log(jax.random.uniform(ks[6], (DEPTH, SSD_HEADS), f32, 1.0, 16.0)),
        "ssd_d": 1.0 + nrm(ks[7], (DEPTH, SSD_HEADS), 0.02),
        "ssd_norm_w": 1.0 + nrm(ks[8], (DEPTH, SSD_WIDTH), 0.02),
        "q_norm_w": 1.0 + nrm(ks[9], (DEPTH, ATTN_HEAD_DIM), 0.02),
        "k_norm_w": 1.0 + nrm(ks[10], (DEPTH, ATTN_HEAD_DIM), 0.02),
        "attn_sinks": nrm(ks[11], (DEPTH, ATTN_Q_HEADS), 0.5),
        "cm_dw_w": nrm(ks[12], (DEPTH, CM_CONV_WIDTH, CM_CHANNELS), CM_CONV_WIDTH ** -0.5),
        "cm_dw_b": nrm(ks[13], (DEPTH, CM_CHANNELS), 0.02),
        "cm_ln_w": 1.0 + nrm(ks[14], (DEPTH, CM_CHANNELS), 0.02),
        "cm_ln_b": nrm(ks[15], (DEPTH, CM_CHANNELS), 0.02),
        "w_out": nrm(ks[16], (DEPTH, D_MIX, D_MODEL), D_MIX ** -0.5),
        "norm_mlp_w": 1.0 + nrm(ks[17], (DEPTH, D_MODEL), 0.02),
        "w_mlp_up": nrm(ks[18], (DEPTH, D_MODEL, D_FF), D_MODEL ** -0.5),
        "w_mlp_down": nrm(ks[19], (DEPTH, D_FF, D_MODEL), D_FF ** -0.5),
    }


def reference(x, norm_mix_w, w_in, ssd_conv_w, ssd_conv_b, ssd_dt_bias, ssd_a_log, ssd_d,
              ssd_norm_w, q_norm_w, k_norm_w, attn_sinks, cm_dw_w, cm_dw_b, cm_ln_w, cm_ln_b,
              w_out, norm_mlp_w, w_mlp_up, w_mlp_down):
    cos, sin = rope_tables(x.shape[1], ATTN_HEAD_DIM)
    for i in range(DEPTH):
        x = hybrid_layer(x, cos, sin, norm_mix_w[i], w_in[i], ssd_conv_w[i], ssd_conv_b[i],
                         ssd_dt_bias[i], ssd_a_log[i], ssd_d[i], ssd_norm_w[i], q_norm_w[i],
                         k_norm_w[i], attn_sinks[i], cm_dw_w[i], cm_dw_b[i], cm_ln_w[i],
                         cm_ln_b[i], w_out[i], norm_mlp_w[i], w_mlp_up[i], w_mlp_down[i])
    return x
```

```python
import numpy as np
from contextlib import ExitStack
import concourse.bass as bass
import concourse.mybir as mybir
from concourse.bass_utils import run_bass_kernel_spmd

F32 = mybir.dt.float32
BF16 = mybir.dt.bfloat16
ALU = mybir.AluOpType
AF = mybir.ActivationFunctionType

NLAYERS = 4
SEQ = 2048
DM = 1024
NMW, NLW, CW, CB, SNW, QW, KW, DWW, DWB, LNW, LNB, NCOL = 0, 8, 16, 64, 76, 84, 85, 86, 210, 214, 218, 222
DTB, ALOG, SINK, DROW, NROW = 0, 16, 32, 40, 56
RMS_EPS = 1e-6
LN_EPS = 1e-5


class Buf:
    __slots__ = ("name", "w", "r")

    def __init__(self, name):
        self.name = name
        self.w = {}
        self.r = {}


class FW:
    def __init__(self, nc, es):
        self.nc, self.es = nc, es
        self.engines = {"pe": nc.tensor, "act": nc.scalar, "dve": nc.vector,
                        "pool": nc.gpsimd, "sp": nc.sync}
        self.sems, self.cnt = {}, {}
        self.seen = {e: {} for e in self.engines}
        for e in ("pe", "act", "dve", "pool"):
            self.sems[e] = es.enter_context(nc.semaphore("sem_" + e))
            self.cnt[e] = 0
        self.nwaits = 0
        self.nops = 0

    def new_dma_sem(self, name):
        self.sems[name] = self.es.enter_context(self.nc.semaphore(name))
        self.cnt[name] = 0
        return name

    def _deps(self, reads, writes):
        d = {}
        for b in reads:
            for k, v in b.w.items():
                if d.get(k, 0) < v:
                    d[k] = v
        for b in writes:
            for ev in (b.w, b.r):
                for k, v in ev.items():
                    if d.get(k, 0) < v:
                        d[k] = v
        return d

    def _wait(self, eng, deps):
        seen = self.seen[eng]
        for k, v in deps.items():
            if eng == "pe" and k == "pe":
                continue
            if seen.get(k, 0) >= v:
                continue
            self.engines[eng].wait_ge(self.sems[k], v)
            seen[k] = v
            self.nwaits += 1

    def op(self, eng, fn, reads=(), writes=()):
        self._wait(eng, self._deps(reads, writes))
        ins = fn(self.engines[eng])
        self.cnt[eng] += 1
        c = self.cnt[eng]
        ins.then_inc(self.sems[eng], 1)
        self.nops += 1
        for b in reads:
            if b.r.get(eng, 0) < c:
                b.r[eng] = c
        for b in writes:
            b.w = {eng: c}
            b.r = {}

    def dma(self, queue, sem, pairs, reads=(), writes=()):
        self._wait(queue, self._deps(reads, writes))
        q = self.engines[queue]
        for (o, i) in pairs:
            q.dma_start(out=o, in_=i).then_inc(self.sems[sem], 16)
            self.cnt[sem] += 16
        v = self.cnt[sem]
        for b in reads:
            b.r[sem] = v
        for b in writes:
            b.w = {sem: v}
            b.r = {}

    def final_wait(self, eng, sems):
        for s in sems:
            self.engines[eng].wait_ge(self.sems[s], self.cnt[s])


class _Stop(Exception):
    pass


def build_program(NL=NLAYERS, NSC=SEQ // 512, dbg=(), stop_after=99):
    L = NSC * 512
    NCH = L // 128
    nc = bass.Bass("TRN2", target_bir_lowering=False)
    dram = nc.dram_tensor
    x_d = dram("x", [L, DM], F32, kind="ExternalInput").ap()
    win_d = dram("w_in", [NL, 1024, 4368], F32, kind="ExternalInput").ap()
    wout_d = dram("w_out", [NL, 2048, 1024], F32, kind="ExternalInput").ap()
    wup_d = dram("w_up", [NL, 1024, 4096], F32, kind="ExternalInput").ap()
    wdn_d = dram("w_down", [NL, 4096, 1024], F32, kind="ExternalInput").ap()
    pcol_d = dram("pcol", [128, NL, NCOL], F32, kind="ExternalInput").ap()
    prow_d = dram("prow", [128, NL, NROW], F32, kind="ExternalInput").ap()
    cst_d = dram("cst", [128, 512], F32, kind="ExternalInput").ap()
    cos_d = dram("cosT", [128, L], F32, kind="ExternalInput").ap()
    sin_d = dram("sinT", [128, L], F32, kind="ExternalInput").ap()
    out_d = dram("out", [L, DM], F32, kind="ExternalOutput").ap()
    dbg_d = {}
    for (name, shape) in dbg:
        dbg_d[name] = dram("dbg_" + name, list(shape), F32, kind="ExternalOutput").ap()

    with ExitStack() as es:
        fw = FW(nc, es)

        def sb(name, shape, dt=F32):
            return es.enter_context(nc.sbuf_tensor("sb_" + name, list(shape), dt))

        def ps(name, shape, dt=F32):
            return es.enter_context(nc.psum_tensor("ps_" + name, list(shape), dt))

        xT = sb("xT", [128, 8, L]); bxT = [Buf("xT%d" % i) for i in range(NSC)]
        cst = sb("cst", [128, 512]); bcst = Buf("cst")
        ident_f, tri_f, gt_f, RT_f = cst[:, 0:128], cst[:, 128:256], cst[:, 256:384], cst[:, 384:512]
        cb16 = sb("cb16", [128, 5, 128], BF16); bcb16 = Buf("cb16")
        ident_b, tri_b, gt_b, ones_b, blk_b = (cb16[:, i, :] for i in range(5))
        mask2 = cb16[:, 1:3, :]
        ones_f = sb("ones_f", [128, 128]); bones_f = Buf("ones_f")
        pcol = sb("pcol", [128, NL, NCOL]); bpcol = Buf("pcol")
        prow = sb("prow", [128, NL, NROW]); bprow = Buf("prow")
        lay = sb("lay", [128, 32]); blay = Buf("lay")
        hT = sb("hT", [128, 8, 512], BF16); bhT = Buf("hT")
        tg1 = sb("tg1", [128, 512])
        stg = [sb("stg%d" % i, [128, 8, 512]) for i in range(2)]; bstg = [Buf("stg%d" % i) for i in range(2)]
        wbf = [sb("wbf%d" % i, [128, 8, 512], BF16) for i in range(2)]; bwbf = [Buf("wbf%d" % i) for i in range(2)]
        yT = sb("yT", [128, 16, 512], BF16); byT = [Buf("yT%d" % i) for i in range(4)]
        hst = sb("hst", [128, 2, 512]); bhst = [Buf("hst0"), Buf("hst1")]
        hstb = sb("hstb", [128, 2, 512], BF16); bhstb = [Buf("hstb0"), Buf("hstb1")]
        R = sb("R", [128, 20, 512]); bR = [Buf("R%d" % i) for i in range(20)]
        Rf = R
        Rb = R[:, :, :].bitcast(BF16)
        tg = [R[:, 19, :], tg1]; btg = [bR[19], Buf("tg1")]
        pc = [sb("pc%d" % i, [128, 515]) for i in range(2)]; bpc = [Buf("pc0"), Buf("pc1")]
        tails = sb("tails", [128, 12, 3]); btails = Buf("tails")
        hbuf = sb("hbuf", [128, 4, 542], BF16); bhbuf = Buf("hbuf")
        kT = sb("kT", [128, 2, 640], BF16); bkT = Buf("kT")
        Vaug = sb("Vaug", [128, 5, 2, 80], BF16); bV = Buf("Vaug")
        dtr = sb("dtr", [128, 4, 16]); bdtr = Buf("dtr")
        adt = sb("adt", [128, 4, 16]); badt = Buf("adt")
        est = sb("est", [128, 24]); best = Buf("est")
        w2 = sb("w2", [128, 8]); bw2 = Buf("w2")
        cbm = sb("cbm", [128, 128], BF16); bcbm = Buf("cbm")
        btok = sb("btok", [128, 128], BF16); bbtok = Buf("btok")
        den = sb("den", [128, 8]); bden = Buf("den")

        P = [ps("P%d" % i, [128, 1024]) for i in range(4)]
        bP = [Buf("bank%d" % i) for i in range(8)]

        def bank(i):
            return P[i // 2][:, (i % 2) * 512:(i % 2 + 1) * 512]

        s_misc = fw.new_dma_sem("s_misc")
        s_stg = [fw.new_dma_sem("s_stg0"), fw.new_dma_sem("s_stg1")]
        s_xin = [fw.new_dma_sem("s_xin0"), fw.new_dma_sem("s_xin1")]
        s_out = [fw.new_dma_sem("s_out0"), fw.new_dma_sem("s_out1")]
        s_cs = fw.new_dma_sem("s_cs")
        s_dbg = fw.new_dma_sem("s_dbg")

        def dump(name, ap, bufs):
            if name in dbg_d:
                fw.dma("sp", s_dbg, [(dbg_d[name], ap)], reads=bufs)

        fw.dma("sp", s_misc, [(cst[:], cst_d), (pcol[:], pcol_d), (prow[:], prow_d)], writes=[bcst, bpcol, bprow])
        fw.op("pool", lambda e: e.memset(ones_f[:], 1.0), writes=[bones_f])
        fw.op("pool", lambda e: e.memset(cb16[:, 3:5, :], 1.0), writes=[bcb16])
        fw.op("pool", lambda e: e.memset(cb16[0:64, 4, 64:128], 0.0), writes=[bcb16])
        fw.op("pool", lambda e: e.memset(cb16[64:128, 4, 0:64], 0.0), writes=[bcb16])
        fw.op("dve", lambda e: e.tensor_copy(cb16[:, 0:3, :], cst[:, 0:384].rearrange("p (a b) -> p a b", a=3)),
              reads=[bcst], writes=[bcb16])
        fw.op("pool", lambda e: e.memset(Vaug[:], 1.0), writes=[bV])

        for c in range(NCH):
            s = c % 2
            xin = Rf[:, 2 * s:2 * s + 2, :].rearrange("p a b -> p (a b)")
            bx = bR[2 * s:2 * s + 2]
            fw.dma("sp", s_xin[s], [(xin, x_d[c * 128:(c + 1) * 128, :])], writes=bx)
            pt = P[s]

            def f(pe, xin=xin, pt=pt):
                for k in range(8):
                    last = pe.transpose(pt[:, k * 128:(k + 1) * 128], xin[:, k * 128:(k + 1) * 128], ident_f)
                return last
            fw.op("pe", f, reads=bx + [bcst], writes=[bP[2 * s], bP[2 * s + 1]])
            eng = "act" if c % 2 == 0 else "dve"
            dst = xT[:, :, c * 128:(c + 1) * 128]
            src = pt[:, :].rearrange("p (k t) -> p k t", k=8)
            if eng == "act":
                fw.op("act", lambda e: e.activation(dst, src, AF.Copy), reads=[bP[2 * s], bP[2 * s + 1]], writes=[bxT[c // 4]])
            else:
                fw.op("dve", lambda e: e.tensor_copy(dst, src), reads=[bP[2 * s], bP[2 * s + 1]], writes=[bxT[c // 4]])

        pieces = []

        def add_piece(srcs, casts=None):
            if casts is None:
                n = sum(s[2] for s in srcs)
                casts = [(0, 0, n)]
            pieces.append((srcs, casts))

        for l in range(NL):
            Win = win_d[l].rearrange("(k p) n -> p k n", p=128)
            Wout = wout_d[l].rearrange("(k p) n -> p k n", p=128)
            Wup = wup_d[l].rearrange("(k p) n -> p k n", p=128)
            Wdn = wdn_d[l].rearrange("(k p) n -> p k n", p=128)
            for sc in range(NSC):
                add_piece([(0, Win[:, :, 3216:3344], 128), (128, Win[:, :, 2560:2576], 16)])
                for g in range(2):
                    add_piece([(0, Win[:, :, g * 512:(g + 1) * 512], 512)])
                    add_piece([(0, Win[:, :, 1024 + g * 512:1024 + (g + 1) * 512], 512)])
                    add_piece([(0, Win[:, :, 2048 + g * 128:2048 + (g + 1) * 128], 128),
                               (128, Win[:, :, 2304 + g * 128:2304 + (g + 1) * 128], 128)])
                add_piece([(0, Win[:, :, 2576:3088], 512)])
                add_piece([(0, Win[:, :, 3088:3216], 128)],
                          casts=[(0, 0, 64), (64, 0, 64), (128, 64, 64), (192, 64, 64)])
                add_piece([(0, Win[:, :, 3344:3856], 512)])
                add_piece([(0, Win[:, :, 3856:4368], 512)])
                for cb_ in range(2):
                    for kg in range(2):
                        add_piece([(0, Wout[:, kg * 8:(kg + 1) * 8, cb_ * 512:(cb_ + 1) * 512], 512)])
                for m in range(8):
                    add_piece([(0, Wup[:, :, m * 512:(m + 1) * 512], 512)])
                for cb_ in range(2):
                    for kg in range(4):
                        add_piece([(0, Wdn[:, kg * 8:(kg + 1) * 8, cb_ * 512:(cb_ + 1) * 512], 512)])
        NP = len(pieces)
        cast_engs = ["pool", "act", "pool", "dve"]
        st = {"dma": 0, "cast": 0, "use": 0}

        def w_dma():
            i = st["dma"]
            if i >= NP:
                return
            s = i % 2
            srcs, _ = pieces[i]
            fw.dma("sp", s_stg[s], [(stg[s][:, :, c0:c0 + n], ap) for (c0, ap, n) in srcs], writes=[bstg[s]])
            st["dma"] += 1

        def w_cast():
            i = st["cast"]
            if i >= NP:
                return
            s = i % 2
            _, casts = pieces[i]
            eng = cast_engs[i % len(cast_engs)]
            for (b0, s0, n) in casts:
                o = wbf[s][:, :, b0:b0 + n]
                a = stg[s][:, :, s0:s0 + n]
                if eng == "act":
                    fw.op("act", lambda e: e.activation(o, a, AF.Copy), reads=[bstg[s]], writes=[bwbf[s]])
                else:
                    fw.op(eng, lambda e: e.tensor_copy(o, a), reads=[bstg[s]], writes=[bwbf[s]])
            st["cast"] += 1

        def w_acquire():
            i = st["use"]
            return wbf[i % 2], bwbf[i % 2]

        def w_release():
            st["use"] += 1
            w_cast()
            w_dma()

        w_dma(); w_dma(); w_cast(); w_dma(); w_cast(); w_dma()

        acc_rr = [0]

        def stage(n):
            if n >= stop_after:
                raise _Stop()

        def next_acc():
            i = acc_rr[0] % 4
            acc_rr[0] += 1
            return i

        def mm_feat(bi, wt, bw, j, nk=8):
            def f(pe):
                for k in range(nk):
                    last = pe.matmul(bank(bi), wt[:, k, j * 128:(j + 1) * 128], hT[:, k, :], start=(k == 0), stop=(k == nk - 1))
                return last
            fw.op("pe", f, reads=[bw, bhT], writes=[bP[bi]])

        def rms_norm_to_hT(sc, wofs, l):
            tok = slice(sc * 512, (sc + 1) * 512)
            sq = yT[:, 0:8, :]
            fw.op("act", lambda e: e.activation(sq, xT[:, :, tok], AF.Square), reads=[bxT[sc]], writes=[byT[0], byT[1]])

            def f(pe):
                for k in range(8):
                    last = pe.matmul(bank(4), ones_b, yT[:, k, :], start=(k == 0), stop=(k == 7))
                return last
            fw.op("pe", f, reads=[byT[0], byT[1], bcb16], writes=[bP[4]])
            fw.op("act", lambda e: e.activation(tg[0], bank(4), AF.Sqrt, bias=RMS_EPS, scale=1.0 / DM), reads=[bP[4]], writes=[btg[0]])
            fw.op("dve", lambda e: e.reciprocal(tg[0], tg[0]), reads=[btg[0]], writes=[btg[0]])
            for k in range(8):
                eng = "dve"
                fw.op(eng, lambda e: e.scalar_tensor_tensor(out=hT[:, k, :], in0=xT[:, k, tok], scalar=pcol[:, l, wofs + k:wofs + k + 1],
                                                            in1=tg[0], op0=ALU.mult, op1=ALU.mult),
                      reads=[bxT[sc], btg[0], bpcol], writes=[bhT])

        try:
          for l in range(NL):
            fw.op("act", lambda e: e.activation(lay[:, 0:16], prow[:, l, ALOG:ALOG + 16], AF.Exp), reads=[bprow], writes=[blay])
            fw.op("dve", lambda e: e.tensor_scalar(lay[:, 0:16], lay[:, 0:16], -1.0, None, ALU.mult), reads=[blay], writes=[blay])
            fw.op("act", lambda e: e.activation(lay[:, 16:24], prow[:, l, SINK:SINK + 8], AF.Exp), reads=[bprow], writes=[blay])
            fw.op("pool", lambda e: e.memset(hst[:], 0.0), writes=bhst)
            fw.op("pool", lambda e: e.memset(hstb[:], 0.0), writes=bhstb)

            for sc in range(NSC):
                tok = slice(sc * 512, (sc + 1) * 512)
                first = (sc == 0)
                rms_norm_to_hT(sc, NMW, l)

                wt, bw = w_acquire()
                for c in range(4):
                    cs = slice(c * 128, (c + 1) * 128)

                    def f(pe):
                        for k in range(8):
                            last = pe.matmul(bank(4)[:, 0:144], hT[:, k, cs], wt[:, k, 0:144], start=(k == 0), stop=(k == 7))
                        return last
                    fw.op("pe", f, reads=[bw, bhT], writes=[bP[4]])
                    fw.op("act", lambda e: e.activation(Vaug[:, 1 + c, :, 0:64], bank(4)[:, 0:128].rearrange("p (g d) -> p g d", g=2), AF.Copy),
                          reads=[bP[4]], writes=[bV])
                    fw.op("dve", lambda e: e.tensor_tensor(out=dtr[:, c, :], in0=bank(4)[:, 128:144], in1=prow[:, l, DTB:DTB + 16], op=ALU.add),
                          reads=[bP[4], bprow], writes=[bdtr])
                w_release()
                fw.op("act", lambda e: e.activation(dtr[:], dtr[:], AF.Exp), reads=[bdtr], writes=[bdtr])
                fw.op("act", lambda e: e.activation(dtr[:], dtr[:], AF.Ln, bias=1.0, scale=1.0), reads=[bdtr], writes=[bdtr])
                fw.op("pool", lambda e: e.tensor_tensor(out=adt[:], in0=dtr[:], in1=lay[:, 0:16].unsqueeze(1).broadcast_to([128, 4, 16]), op=ALU.mult),
                      reads=[bdtr, blay], writes=[badt])

                stage(1)
                zsT = Rf[:, 0:4, :]; bz = bR[0:4]
                xsT = Rf[:, 4:8, :]; bxs = bR[4:8]
                BT = Rb[:, 8, 0:512]; CT = Rb[:, 8, 512:1024]; bBC = bR[8]
                accb = [Rf[:, 9, :], Rf[:, 10, :]]; baccb = [bR[9], bR[10]]
                rseg = Rf[:, 11:13, :].rearrange("p a b -> p (a b)").rearrange("p (h l) -> p h l", h=8); brseg = bR[11:13]
                dec = Rb[:, 13, :]; dec3 = dec.rearrange("p (h l) -> p h l", h=8); bdec = bR[13]
                xdt = Rb[:, 14, 0:512]; xdtd = Rb[:, 14, 512:1024]; bxdt = bR[14]
                xsD = Rf[:, 15, :]; bxsD = bR[15]
                t1 = Rf[:, 16, :]; bt1 = bR[16]
                ytok = Rf[:, 17, :]; bytok = bR[17]
                sqg = Rb[:, 18:20, :].rearrange("p a b -> p (a b)").rearrange("p (j t) -> p j t", j=4); bsqg = bR[18:20]
                conv_i = [0]

                def conv_chunk(bi, cj, dest, bdest, dest_is_list=False):
                    i = conv_i[0] % 2
                    conv_i[0] += 1
                    p, bp = pc[i], bpc[i]
                    a, ba = accb[i], baccb[i]
                    if first:
                        fw.op("pool", lambda e: e.memset(p[:, 0:3], 0.0), writes=[bp])
                    else:
                        fw.op("pool", lambda e: e.tensor_copy(p[:, 0:3], tails[:, cj, :]), reads=[btails], writes=[bp])
                    fw.op("act", lambda e: e.activation(p[:, 3:515], bank(bi), AF.Copy), reads=[bP[bi]], writes=[bp])
                    fw.op("pool", lambda e: e.tensor_copy(tails[:, cj, :], p[:, 512:515]), reads=[bp], writes=[btails])
                    co = CW + cj * 4
                    fw.op("act", lambda e: e.activation(a, p[:, 0:512], AF.Identity, bias=pcol[:, l, CB + cj:CB + cj + 1], scale=pcol[:, l, co:co + 1]),
                          reads=[bp, bpcol], writes=[ba])
                    for k in range(1, 4):
                        eng = "dve"
                        fw.op(eng, lambda e: e.scalar_tensor_tensor(out=a, in0=p[:, k:k + 512], scalar=pcol[:, l, co + k:co + k + 1], in1=a,
                                                                    op0=ALU.mult, op1=ALU.add),
                              reads=[bp, bpcol, ba], writes=[ba])
                    fw.op("act", lambda e: e.activation(dest, a, AF.Silu), reads=[ba], writes=bdest)

                for g in range(2):
                    hs = slice(g * 8, (g + 1) * 8)
                    wt, bw = w_acquire()
                    for j in range(4):
                        bi = next_acc()
                        mm_feat(bi, wt, bw, j)
                        fw.op("act", lambda e: e.activation(zsT[:, j, :], bank(bi), AF.Silu), reads=[bP[bi]], writes=[bz[j]])
                    w_release()
                    wt, bw = w_acquire()
                    for j in range(4):
                        bi = next_acc()
                        mm_feat(bi, wt, bw, j)
                        conv_chunk(bi, g * 4 + j, xsT[:, j, :], [bxs[j]])
                    w_release()
                    wt, bw = w_acquire()
                    for j in range(2):
                        bi = next_acc()
                        mm_feat(bi, wt, bw, j)
                        conv_chunk(bi, 8 + 2 * j + g, BT if j == 0 else CT, [bBC])
                    w_release()

                    stage(2 if g == 0 else 3.5)
                    for c in range(4):
                        cs = slice(c * 128, (c + 1) * 128)
                        adt_c = adt[:, c, hs]
                        dt_c = dtr[:, c, hs]

                        def f(pe):
                            pe.matmul(bank(4)[:, 0:8], tri_f, adt_c, start=True, stop=True)
                            pe.matmul(bank(4)[:, 8:16], gt_f, adt_c, start=True, stop=True)
                            return pe.matmul(bank(4)[:, 16:24], ones_f[:], adt_c, start=True, stop=True)
                        fw.op("pe", f, reads=[badt, bcst, bones_f], writes=[bP[4]])
                        fw.op("act", lambda e: e.activation(est[:], bank(4)[:, 0:24], AF.Exp), reads=[bP[4]], writes=[best])
                        fw.op("pool", lambda e: e.tensor_tensor(out=rseg, in0=tri_f.unsqueeze(1).broadcast_to([128, 8, 128]),
                                                                 in1=adt_c.unsqueeze(2).broadcast_to([128, 8, 128]), op=ALU.mult),
                              reads=[bcst, badt], writes=brseg)

                        def f(pe):
                            pe.matmul(P[3][:, 0:512], gt_f, rseg[:, 0:4, :], start=True, stop=True)
                            return pe.matmul(P[3][:, 512:1024], gt_f, rseg[:, 4:8, :], start=True, stop=True)
                        fw.op("pe", f, reads=brseg + [bcst], writes=[bP[6], bP[7]])
                        fw.op("act", lambda e: e.activation(dec, P[3][:, :], AF.Exp), reads=[bP[6], bP[7]], writes=[bdec])
                        fw.op("pe", lambda pe: pe.matmul(bank(4)[:, 64:192], BT[:, cs], CT[:, cs], start=True, stop=True), reads=[bBC], writes=[bP[4]])
                        fw.op("dve", lambda e: e.tensor_tensor(out=cbm[:], in0=bank(4)[:, 64:192], in1=tri_f, op=ALU.mult),
                              reads=[bP[4], bcst], writes=[bcbm])
                        fw.op("dve", lambda e: e.tensor_tensor(out=dec3, in0=dec3, in1=cbm[:].unsqueeze(1).broadcast_to([128, 8, 128]), op=ALU.mult),
                              reads=[bdec, bcbm], writes=[bdec])

                        def f(pe):
                            for j in range(4):
                                last = pe.transpose(bank(5)[:, j * 128:(j + 1) * 128], xsT[:, j, cs], ident_f)
                            return last
                        fw.op("pe", f, reads=bxs + [bcst], writes=[bP[5]])
                        xs3 = bank(5).rearrange("p (h d) -> p h d", h=8)
                        fw.op("dve", lambda e: e.tensor_tensor(out=xdt.rearrange("p (h d) -> p h d", h=8), in0=xs3,
                                                                in1=dt_c.unsqueeze(2).broadcast_to([128, 8, 64]), op=ALU.mult),
                              reads=[bP[5], bdtr], writes=[bxdt])
                        fw.op("pool", lambda e: e.tensor_tensor(out=w2[:], in0=dt_c, in1=est[:, 8:16], op=ALU.mult), reads=[bdtr, best], writes=[bw2])
                        fw.op("dve", lambda e: e.tensor_tensor(out=xdtd.rearrange("p (h d) -> p h d", h=8), in0=xs3,
                                                                in1=w2[:].unsqueeze(2).broadcast_to([128, 8, 64]), op=ALU.mult),
                              reads=[bP[5], bw2], writes=[bxdt])
                        fw.op("dve", lambda e: e.tensor_tensor(out=xsD.rearrange("p (h d) -> p h d", h=8), in0=xs3,
                                                                in1=prow[:, l, DROW + g * 8:DROW + (g + 1) * 8].unsqueeze(2).broadcast_to([128, 8, 64]), op=ALU.mult),
                              reads=[bP[5], bprow], writes=[bxsD])

                        def f(pe):
                            for h in range(8):
                                pe.matmul(bank(0)[:, h * 64:(h + 1) * 64], dec3[:, h, :], xdt[:, h * 64:(h + 1) * 64], start=True, stop=True)
                            return pe.matmul(bank(1), CT[:, cs], hstb[:, g, :], start=True, stop=True)
                        fw.op("pe", f, reads=[bdec, bxdt, bBC, bhstb[g]], writes=[bP[0], bP[1]])
                        fw.op("dve", lambda e: e.tensor_tensor(out=t1.rearrange("p (h d) -> p h d", h=8), in0=bank(1).rearrange("p (h d) -> p h d", h=8),
                                                                in1=est[:, 0:8].unsqueeze(2).broadcast_to([128, 8, 64]), op=ALU.mult),
                              reads=[bP[1], best], writes=[bt1])
                        fw.op("dve", lambda e: e.tensor_tensor(out=ytok, in0=bank(0), in1=t1, op=ALU.add), reads=[bP[0], bt1], writes=[bytok])
                        fw.op("pool", lambda e: e.tensor_tensor(out=ytok, in0=ytok, in1=xsD, op=ALU.add), reads=[bytok, bxsD], writes=[bytok])

                        def f(pe):
                            for j in range(4):
                                last = pe.transpose(bank(3)[:, j * 128:(j + 1) * 128], ytok[:, j * 128:(j + 1) * 128], ident_f)
                            return last
                        fw.op("pe", f, reads=[bytok, bcst], writes=[bP[3]])
                        fw.op("dve", lambda e: e.tensor_tensor(out=zsT[:, :, cs], in0=bank(3).rearrange("p (j t) -> p j t", j=4), in1=zsT[:, :, cs], op=ALU.mult),
                              reads=[bP[3]] + bz, writes=bz)
                        fw.op("pe", lambda pe: pe.matmul(bank(4)[:, 192:320], BT[:, cs], ident_b, start=True, stop=True), reads=[bBC, bcb16], writes=[bP[4]])
                        fw.op("act", lambda e: e.activation(btok[:], bank(4)[:, 192:320], AF.Copy), reads=[bP[4]], writes=[bbtok])
                        fw.op("pe", lambda pe: pe.matmul(bank(2), btok[:], xdtd, start=True, stop=True), reads=[bbtok, bxdt], writes=[bP[2]])
                        hg = hst[:, g, :]
                        fw.op("pool", lambda e: e.tensor_tensor(out=hg.rearrange("p (h d) -> p h d", h=8), in0=hg.rearrange("p (h d) -> p h d", h=8),
                                                                 in1=est[:, 16:24].unsqueeze(2).broadcast_to([128, 8, 64]), op=ALU.mult),
                              reads=[bhst[g], best], writes=[bhst[g]])
                        fw.op("dve", lambda e: e.tensor_tensor(out=hg, in0=bank(2), in1=hg, op=ALU.add), reads=[bP[2], bhst[g]], writes=[bhst[g]])
                        fw.op("act", lambda e: e.activation(hstb[:, g, :], hg, AF.Copy), reads=[bhst[g]], writes=[bhstb[g]])

                    stage(2.5 if g == 0 else 3.7)
                    fw.op("act", lambda e: e.activation(sqg, zsT, AF.Square), reads=bz, writes=bsqg)

                    def f(pe):
                        for j in range(4):
                            last = pe.matmul(bank(5), ones_b, sqg[:, j, :], start=(j == 0), stop=(j == 3))
                        return last
                    fw.op("pe", f, reads=bsqg + [bcb16], writes=[bP[5]])
                    fw.op("act", lambda e: e.activation(tg[1][:, :], bank(5), AF.Sqrt, bias=RMS_EPS, scale=1.0 / 512), reads=[bP[5]], writes=[btg[1]])
                    fw.op("dve", lambda e: e.reciprocal(tg[1][:, :], tg[1][:, :]), reads=[btg[1]], writes=[btg[1]])
                    for j in range(4):
                        eng = "dve"
                        so = SNW + g * 4 + j
                        fw.op(eng, lambda e: e.scalar_tensor_tensor(out=yT[:, g * 4 + j, :], in0=zsT[:, j, :], scalar=pcol[:, l, so:so + 1], in1=tg[1][:, :],
                                                                    op0=ALU.mult, op1=ALU.mult),
                              reads=[bz[j], btg[1], bpcol], writes=[byT[g]])

                stage(4)
                qTb = Rb[:, 0:2, :]
                bq = bR[0:2]
                qraw = Rf[:, 2, :]; qn = Rf[:, 3, :]; ta = Rf[:, 4, :]; tb = Rf[:, 5, :]
                cosS = Rf[:, 6, :]; sinS = Rf[:, 7, :]
                sqq = Rb[:, 8, 0:512]
                rsq = Rf[:, 9, :]
                PT = [Rb[:, 10, :], Rb[:, 11, :]]
                yat = Rf[:, 13, :]
                fw.dma("sp", s_cs, [(cosS, cos_d[:, tok]), (sinS, sin_d[:, tok])], writes=[bR[6], bR[7]])
                qhi = Rb[:, 14:16, :]
                bqh = bR[14:16]
                fw.op("pool", lambda e: e.memset(qTb, 0.0), writes=bq)
                fw.op("pool", lambda e: e.memset(qhi, 0.0), writes=bqh)
                for i in range(6):
                    if i == 0 or i == 4:
                        wt, bw = w_acquire()
                    j = i if i < 4 else i - 4
                    bi = next_acc()
                    mm_feat(bi, wt, bw, j)
                    fw.op("act", lambda e: e.activation(qraw, bank(bi), AF.Copy), reads=[bP[bi]], writes=[bR[2]])
                    fw.op("act", lambda e: e.activation(sqq, bank(bi), AF.Square), reads=[bP[bi]], writes=[bR[8]])
                    fw.op("pe", lambda pe: pe.matmul(bank(5), blk_b, sqq, start=True, stop=True), reads=[bR[8], bcb16], writes=[bP[5]])
                    fw.op("act", lambda e: e.activation(rsq, bank(5), AF.Sqrt, bias=RMS_EPS, scale=1.0 / 64), reads=[bP[5]], writes=[bR[9]])
                    fw.op("dve", lambda e: e.reciprocal(rsq, rsq), reads=[bR[9]], writes=[bR[9]])
                    wo = QW if i < 4 else KW
                    fw.op("dve", lambda e: e.scalar_tensor_tensor(out=qn, in0=qraw, scalar=pcol[:, l, wo:wo + 1], in1=rsq, op0=ALU.mult, op1=ALU.mult),
                          reads=[bR[2], bR[9], bpcol], writes=[bR[3]])
                    fw.op("pe", lambda pe: pe.matmul(bank(6), RT_f, qn, start=True, stop=True), reads=[bR[3], bcst], writes=[bP[6]])
                    fw.op("pool", lambda e: e.tensor_tensor(out=ta, in0=qn, in1=cosS, op=ALU.mult), reads=[bR[3], bR[6]], writes=[bR[4]])
                    fw.op("dve", lambda e: e.tensor_tensor(out=tb, in0=bank(6), in1=sinS, op=ALU.mult), reads=[bP[6], bR[7]], writes=[bR[5]])
                    if i < 4:
                        cc = slice((i % 2) * 512, (i % 2 + 1) * 512)
                        fw.op("pool", lambda e: e.tensor_tensor(out=qTb[0:64, i // 2, cc], in0=ta[0:64, :], in1=tb[0:64, :], op=ALU.add),
                              reads=[bR[4], bR[5]], writes=[bq[i // 2]])
                        fw.op("dve", lambda e: e.tensor_tensor(out=qhi[64:128, i // 2, cc], in0=ta[64:128, :], in1=tb[64:128, :], op=ALU.add),
                              reads=[bR[4], bR[5]], writes=[bqh[i // 2]])
                    else:
                        dst = kT[:, i - 4, 128:640]
                        fw.op("pool", lambda e: e.tensor_tensor(out=dst, in0=ta, in1=tb, op=ALU.add), reads=[bR[4], bR[5]], writes=[bkT])
                    if i == 3 or i == 5:
                        w_release()

                stage(5)
                for c in range(4):
                    cs = slice(c * 128, (c + 1) * 128)
                    gc = sc * 4 + c
                    kcs = [1] if gc == 0 else [0, 1]
                    for g in range(2):
                        PS = P[3][:, :].rearrange("p (kc jj par q) -> p kc jj par q", kc=2, jj=2, par=2)

                        def f(pe):
                            for kc in kcs:
                                k0 = c * 128 + (128 if kc == 1 else 0)
                                for par in range(2):
                                    qsrc = qTb if par == 0 else qhi
                                    for jj in range(2):
                                        rhs = qsrc[:, g, jj * 512 + c * 128:jj * 512 + (c + 1) * 128]
                                        last = pe.matmul(PS[:, kc, jj, par, :], kT[:, g, k0:k0 + 128], rhs, start=True, stop=True)
                            return last
                        fw.op("pe", f, reads=bq + bqh + [bkT], writes=[bP[6], bP[7]])
                        pt4 = PT[g].rearrange("p (kc r q) -> p kc r q", kc=2, r=4)
                        ps4 = P[3][:, :].rearrange("p (kc r q) -> p kc r q", kc=2, r=4)
                        k_lo = kcs[0]
                        fw.op("act", lambda e: e.activation(PT[g][:, k_lo * 512:1024], P[3][:, k_lo * 512:1024], AF.Exp, scale=0.125),
                              reads=[bP[6], bP[7]], writes=[bR[10 + g]])
                        fw.op("dve", lambda e: e.tensor_tensor(out=pt4[:, 1], in0=pt4[:, 1], in1=tri_b.unsqueeze(1).broadcast_to([128, 4, 128]), op=ALU.mult),
                              reads=[bR[10 + g], bcb16], writes=[bR[10 + g]])
                        if gc > 0:
                            fw.op("pool", lambda e: e.tensor_tensor(out=pt4[:, 0], in0=pt4[:, 0], in1=gt_b.unsqueeze(1).broadcast_to([128, 4, 128]), op=ALU.mult),
                                  reads=[bR[10 + g], bcb16], writes=[bR[10 + g]])
                        ob = g
                        O = bank(ob).rearrange("p (r e) -> p r e", r=4)

                        def f(pe):
                            for r in range(4):
                                jj, par = r // 2, r % 2
                                for kc in kcs:
                                    last = pe.matmul(O[:, r, 0:72], pt4[:, kc, jj * 2 + par, :], Vaug[:, c + kc, g, 0:72],
                                                     start=(kc == kcs[0]), stop=(kc == 1))
                            return last
                        fw.op("pe", f, reads=[bR[10 + g], bV], writes=[bP[ob]])
                        fw.op("dve", lambda e: e.tensor_tensor(out=den[:, g * 4:(g + 1) * 4], in0=O[:, :, 64], in1=lay[:, 16 + g * 4:16 + (g + 1) * 4], op=ALU.add),
                              reads=[bP[ob], blay], writes=[bden])
                        fw.op("dve", lambda e: e.reciprocal(den[:, g * 4:(g + 1) * 4], den[:, g * 4:(g + 1) * 4]), reads=[bden], writes=[bden])
                        fw.op("dve", lambda e: e.tensor_tensor(out=yat.rearrange("p (h d) -> p h d", h=8)[:, g * 4:(g + 1) * 4, :], in0=O[:, :, 0:64],
                                                                in1=den[:, g * 4:(g + 1) * 4].unsqueeze(2).broadcast_to([128, 4, 64]), op=ALU.mult),
                              reads=[bP[ob], bden], writes=[bR[13]])

                    def f(pe):
                        for j in range(4):
                            last = pe.transpose(bank(2)[:, j * 128:(j + 1) * 128], yat[:, j * 128:(j + 1) * 128], ident_f)
                        return last
                    fw.op("pe", f, reads=[bR[13], bcst], writes=[bP[2]])
                    fw.op("act", lambda e: e.activation(yT[:, 8:12, cs], bank(2).rearrange("p (j t) -> p j t", j=4), AF.Copy), reads=[bP[2]], writes=[byT[2]])
                fw.op("pool", lambda e: e.tensor_copy(kT[:, :, 0:128], kT[:, :, 512:640]), reads=[bkT], writes=[bkT])
                fw.op("pool", lambda e: e.tensor_copy(Vaug[:, 0], Vaug[:, 4]), reads=[bV], writes=[bV])

                stage(6)
                aT = Rf[:, 0:4, :]; ba = bR[0:4]
                cT = Rf[:, 4:8, :]; bc = bR[4:8]
                sig = Rf[:, 8, :]
                dgs = Rb[:, 9:13, :].rearrange("p a b -> p (a b)")[:, 0:3968].rearrange("p (k c) -> p k c", c=128); bdg = bR[9:13]
                sqc = Rb[:, 13:15, :].rearrange("p a b -> p (a b)").rearrange("p (j t) -> p j t", j=4); bsqc = bR[13:15]
                mean = Rf[:, 17, :]; rstd = Rf[:, 18, :]; msq = Rf[:, 19, :]
                if first:
                    fw.op("pool", lambda e: e.memset(hbuf[:, :, 0:30], 0.0), writes=[bhbuf])
                else:
                    fw.op("pool", lambda e: e.tensor_copy(hbuf[:, :, 0:30], hbuf[:, :, 512:542]), reads=[bhbuf], writes=[bhbuf])
                wt, bw = w_acquire()
                for j in range(4):
                    bi = next_acc()
                    mm_feat(bi, wt, bw, j)
                    fw.op("act", lambda e: e.activation(aT[:, j, :], bank(bi), AF.Copy), reads=[bP[bi]], writes=[ba[j]])
                w_release()
                wt, bw = w_acquire()
                for j in range(4):
                    bi = next_acc()
                    mm_feat(bi, wt, bw, j)
                    fw.op("act", lambda e: e.activation(sig, bank(bi), AF.Sigmoid), reads=[bP[bi]], writes=[bR[8]])
                    fw.op("dve", lambda e: e.tensor_tensor(out=hbuf[:, j, 30:542], in0=aT[:, j, :], in1=sig, op=ALU.mult), reads=[ba[j], bR[8]], writes=[bhbuf])
                w_release()
                for j in range(4):
                    wofs = DWW + j * 31
                    fw.op("pool", lambda e: e.tensor_tensor(out=dgs, in0=ident_f.unsqueeze(1).broadcast_to([128, 31, 128]),
                                                             in1=pcol[:, l, wofs:wofs + 31].unsqueeze(2).broadcast_to([128, 31, 128]), op=ALU.mult),
                          reads=[bcst, bpcol], writes=bdg)
                    bi = next_acc()

                    def f(pe):
                        for k in range(31):
                            last = pe.matmul(bank(bi), dgs[:, k, :], hbuf[:, j, k:k + 512], start=(k == 0), stop=(k == 30))
                        return last
                    fw.op("pe", f, reads=bdg + [bhbuf], writes=[bP[bi]])
                    bcol = pcol[:, l, DWB + j:DWB + j + 1]
                    fw.op("act", lambda e: e.activation(cT[:, j, :], bank(bi), AF.Identity, bias=bcol, scale=1.0), reads=[bP[bi], bpcol], writes=[bc[j]])
                    fw.op("act", lambda e: e.activation(sqc[:, j, :], bank(bi), AF.Square, bias=bcol, scale=1.0), reads=[bP[bi], bpcol], writes=bsqc)

                def f(pe):
                    for j in range(4):
                        pe.matmul(bank(5), ones_f[:], cT[:, j, :], start=(j == 0), stop=(j == 3))
                    for j in range(4):
                        last = pe.matmul(bank(6), ones_b, sqc[:, j, :], start=(j == 0), stop=(j == 3))
                    return last
                fw.op("pe", f, reads=bc + bsqc + [bones_f, bcb16], writes=[bP[5], bP[6]])
                fw.op("dve", lambda e: e.tensor_scalar(mean, bank(5), 1.0 / 512, None, ALU.mult), reads=[bP[5]], writes=[bR[17]])
                fw.op("pool", lambda e: e.tensor_tensor(out=msq, in0=mean, in1=mean, op=ALU.mult), reads=[bR[17]], writes=[bR[19]])
                fw.op("dve", lambda e: e.scalar_tensor_tensor(out=rstd, in0=bank(6), scalar=1.0 / 512, in1=msq, op0=ALU.mult, op1=ALU.subtract),
                      reads=[bP[6], bR[19]], writes=[bR[18]])
                fw.op("act", lambda e: e.activation(rstd, rstd, AF.Sqrt, bias=LN_EPS, scale=1.0), reads=[bR[18]], writes=[bR[18]])
                fw.op("dve", lambda e: e.reciprocal(rstd, rstd), reads=[bR[18]], writes=[bR[18]])
                for j in range(4):
                    fw.op("pool", lambda e: e.tensor_tensor(out=cT[:, j, :], in0=cT[:, j, :], in1=mean, op=ALU.subtract), reads=[bc[j], bR[17]], writes=[bc[j]])
                    fw.op("dve", lambda e: e.tensor_tensor(out=cT[:, j, :], in0=cT[:, j, :], in1=rstd, op=ALU.mult), reads=[bc[j], bR[18]], writes=[bc[j]])
                    fw.op("act", lambda e: e.activation(yT[:, 12 + j, :], cT[:, j, :], AF.Silu, bias=pcol[:, l, LNB + j:LNB + j + 1], scale=pcol[:, l, LNW + j:LNW + j + 1]),
                          reads=[bc[j], bpcol], writes=[byT[3]])

                if "yT" in dbg_d and l == 0 and sc == 0:
                    fw.op("act", lambda e: e.activation(Rf[:, 0:16, :], yT[:, :, :], AF.Copy), reads=byT, writes=bR[0:16])
                    dump("yT", Rf[:, 0:16, :], bR[0:16])

                stage(7)
                def proj_residual(nkg, rhs_of):
                    for cb_ in range(2):
                        for kg in range(nkg):
                            wt, bw = w_acquire()

                            def f(pe):
                                for j in range(4):
                                    for kk in range(8):
                                        last = pe.matmul(bank(j), wt[:, kk, j * 128:(j + 1) * 128], rhs_of(kg * 8 + kk),
                                                         start=(kg == 0 and kk == 0), stop=(kg == nkg - 1 and kk == 7))
                                return last
                            fw.op("pe", f, reads=[bw] + rhs_bufs, writes=bP[0:4])
                            w_release()
                        for j in range(4):
                            xs_ = xT[:, cb_ * 4 + j, tok]
                            fw.op("dve", lambda e: e.tensor_tensor(out=xs_, in0=bank(j), in1=xs_, op=ALU.add), reads=[bP[j], bxT[sc]], writes=[bxT[sc]])

                rhs_bufs = byT
                proj_residual(2, lambda m: yT[:, m, :])

                stage(8)
                rms_norm_to_hT(sc, NLW, l)
                actb = Rb[:, 0:16, :]

                def act_chunk(m):
                    return actb[:, m // 2, (m % 2) * 512:(m % 2 + 1) * 512]
                for m in range(8):
                    wt, bw = w_acquire()
                    for j in range(4):
                        bi = next_acc()
                        mm_feat(bi, wt, bw, j)
                        ch = m * 4 + j
                        tmp, btmp = (Rf[:, 16 + (j % 2), :], bR[16 + (j % 2)])
                        fw.op("act", lambda e: e.activation(tmp, bank(bi), AF.Relu), reads=[bP[bi]], writes=[btmp])
                        eng = "dve" if j % 2 == 0 else "pool"
                        fw.op(eng, lambda e: e.tensor_tensor(out=act_chunk(ch), in0=tmp, in1=tmp, op=ALU.mult), reads=[btmp], writes=[bR[ch // 2]])
                    w_release()
                rhs_bufs = bR[0:16]
                proj_residual(4, act_chunk)

                if "x_after" in dbg_d and l == 0 and sc == 0:
                    dump("x_after", xT[:, :, 0:512], [bxT[0]])

        except _Stop:
            pass
        for c in range(NCH):
            s = c % 2
            pt = P[s]

            def f(pe):
                for k in range(8):
                    last = pe.transpose(pt[:, k * 128:(k + 1) * 128], xT[:, k, c * 128:(c + 1) * 128], ident_f)
                return last
            fw.op("pe", f, reads=[bxT[c // 4], bcst], writes=[bP[2 * s], bP[2 * s + 1]])
            xo = Rf[:, 2 * s:2 * s + 2, :].rearrange("p a b -> p (a b)")
            bx = bR[2 * s:2 * s + 2]
            if c % 2 == 0:
                fw.op("act", lambda e: e.activation(xo, pt[:, :], AF.Copy), reads=[bP[2 * s], bP[2 * s + 1]], writes=bx)
            else:
                fw.op("dve", lambda e: e.tensor_copy(xo, pt[:, :]), reads=[bP[2 * s], bP[2 * s + 1]], writes=bx)
            fw.dma("sp", s_out[s], [(out_d[c * 128:(c + 1) * 128, :], xo)], reads=bx)
        fw.final_wait("sp", [s_out[0], s_out[1], s_dbg, s_stg[0], s_stg[1], s_cs, s_misc, s_xin[0], s_xin[1]])
        print("program: ops=%d waits=%d pieces=%d sbuf_left=%d" % (fw.nops, fw.nwaits, NP, nc.sbuf_bytes_remaining))
    return nc


def host_consts(L):
    cst = np.zeros((128, 512), np.float32)
    i = np.arange(128)
    cst[:, 0:128] = np.eye(128, dtype=np.float32)
    cst[:, 128:256] = (i[:, None] <= i[None, :])
    cst[:, 256:384] = (i[:, None] > i[None, :])
    RT = np.zeros((128, 128), np.float32)
    for blk in (0, 64):
        for d in range(32):
            RT[blk + d + 32, blk + d] = -1.0
            RT[blk + d, blk + d + 32] = 1.0
    cst[:, 384:512] = RT
    inv_freq = (10000.0 ** (-np.arange(0, 64, 2, dtype=np.float32) / 64)).astype(np.float32)
    ang = np.arange(L, dtype=np.float32)[:, None] * inv_freq[None, :]
    cos = np.cos(ang).astype(np.float32).T
    sin = np.sin(ang).astype(np.float32).T
    cosT = np.ascontiguousarray(np.tile(cos, (4, 1)))
    sinT = np.ascontiguousarray(np.tile(sin, (4, 1)))
    return cst, cosT, sinT


def host_params(NL, norm_mix_w, ssd_conv_w, ssd_conv_b, ssd_dt_bias, ssd_a_log, ssd_d, ssd_norm_w, q_norm_w, k_norm_w,
                attn_sinks, cm_dw_w, cm_dw_b, cm_ln_w, cm_ln_b, norm_mlp_w):
    pcol = np.zeros((128, NL, NCOL), np.float32)
    prow = np.zeros((128, NL, NROW), np.float32)
    for l in range(NL):
        pcol[:, l, NMW:NMW + 8] = norm_mix_w[l].reshape(8, 128).T
        pcol[:, l, NLW:NLW + 8] = norm_mlp_w[l].reshape(8, 128).T
        pcol[:, l, CW:CW + 48] = ssd_conv_w[l].reshape(4, 12, 128).transpose(2, 1, 0).reshape(128, 48)
        pcol[:, l, CB:CB + 12] = ssd_conv_b[l].reshape(12, 128).T
        pcol[:, l, SNW:SNW + 8] = ssd_norm_w[l].reshape(8, 128).T
        pcol[:, l, QW] = np.tile(q_norm_w[l], 2)
        pcol[:, l, KW] = np.tile(k_norm_w[l], 2)
        pcol[:, l, DWW:DWW + 124] = cm_dw_w[l].reshape(31, 4, 128).transpose(2, 1, 0).reshape(128, 124)
        pcol[:, l, DWB:DWB + 4] = cm_dw_b[l].reshape(4, 128).T
        pcol[:, l, LNW:LNW + 4] = cm_ln_w[l].reshape(4, 128).T
        pcol[:, l, LNB:LNB + 4] = cm_ln_b[l].reshape(4, 128).T
        prow[:, l, DTB:DTB + 16] = ssd_dt_bias[l][None, :]
        prow[:, l, ALOG:ALOG + 16] = ssd_a_log[l][None, :]
        prow[:, l, SINK:SINK + 8] = attn_sinks[l][None, :]
        prow[:, l, DROW:DROW + 16] = ssd_d[l][None, :]
    return pcol, prow


_CACHE = {}


def kernel(x, norm_mix_w, w_in, ssd_conv_w, ssd_conv_b, ssd_dt_bias, ssd_a_log, ssd_d, ssd_norm_w, q_norm_w, k_norm_w,
           attn_sinks, cm_dw_w, cm_dw_b, cm_ln_w, cm_ln_b, w_out, norm_mlp_w, w_mlp_up, w_mlp_down):
    f = lambda a: np.ascontiguousarray(np.asarray(a, dtype=np.float32))
    x = f(x)
    B, L, _ = x.shape
    NL = w_in.shape[0]
    pcol, prow = host_params(NL, f(norm_mix_w), f(ssd_conv_w), f(ssd_conv_b), f(ssd_dt_bias), f(ssd_a_log), f(ssd_d), f(ssd_norm_w),
                             f(q_norm_w), f(k_norm_w), f(attn_sinks), f(cm_dw_w), f(cm_dw_b), f(cm_ln_w), f(cm_ln_b), f(norm_mlp_w))
    cst, cosT, sinT = host_consts(L)
    key = (NL, L // 512)
    if key not in _CACHE:
        _CACHE[key] = build_program(NL, L // 512)
    nc = _CACHE[key]
    shared = {"w_in": f(w_in), "w_out": f(w_out), "w_up": f(w_mlp_up), "w_down": f(w_mlp_down),
              "pcol": pcol, "prow": prow, "cst": cst, "cosT": cosT, "sinT": sinT}
    in_maps = [dict(shared, x=np.ascontiguousarray(x[b])) for b in range(B)]
    res = run_bass_kernel_spmd(nc, in_maps, core_ids=list(range(B)))
    return np.stack([np.asarray(r["out"], dtype=np.float32) for r in res.results], axis=0)
```

```python
import numpy as np
from contextlib import ExitStack
import concourse.bass as bass
import concourse.mybir as mybir
from concourse.bass_utils import run_bass_kernel_spmd

F32 = mybir.dt.float32
BF16 = mybir.dt.bfloat16
ALU = mybir.AluOpType
AF = mybir.ActivationFunctionType

NLAYERS = 4
SEQ = 2048
DM = 1024
NMW, NLW, CW, CB, SNW, QW, KW, DWW, DWB, LNW, LNB, NCOL = 0, 8, 16, 64, 76, 84, 85, 86, 210, 214, 218, 222
DTB, ALOG, SINK, DROW, NROW = 0, 16, 32, 40, 56
RMS_EPS = 1e-6
LN_EPS = 1e-5


class Buf:
    __slots__ = ("name", "w", "r")

    def __init__(self, name):
        self.name = name
        self.w = {}
        self.r = {}


class FW:
    def __init__(self, nc, es):
        self.nc, self.es = nc, es
        self.engines = {"pe": nc.tensor, "act": nc.scalar, "dve": nc.vector,
                        "pool": nc.gpsimd, "sp": nc.sync}
        self.sems, self.cnt = {}, {}
        self.seen = {e: {} for e in self.engines}
        for e in ("pe", "act", "dve", "pool"):
            self.sems[e] = es.enter_context(nc.semaphore("sem_" + e))
            self.cnt[e] = 0
        self.nwaits = 0
        self.nops = 0

    def new_dma_sem(self, name):
        self.sems[name] = self.es.enter_context(self.nc.semaphore(name))
        self.cnt[name] = 0
        return name

    def _deps(self, reads, writes):
        d = {}
        for b in reads:
            for k, v in b.w.items():
                if d.get(k, 0) < v:
                    d[k] = v
        for b in writes:
            for ev in (b.w, b.r):
                for k, v in ev.items():
                    if d.get(k, 0) < v:
                        d[k] = v
        return d

    def _wait(self, eng, deps):
        seen = self.seen[eng]
        for k, v in deps.items():
            if eng == "pe" and k == "pe":
                continue
            if seen.get(k, 0) >= v:
                continue
            self.engines[eng].wait_ge(self.sems[k], v)
            seen[k] = v
            self.nwaits += 1

    def op(self, eng, fn, reads=(), writes=()):
        self._wait(eng, self._deps(reads, writes))
        ins = fn(self.engines[eng])
        self.cnt[eng] += 1
        c = self.cnt[eng]
        ins.then_inc(self.sems[eng], 1)
        self.nops += 1
        for b in reads:
            if b.r.get(eng, 0) < c:
                b.r[eng] = c
        for b in writes:
            b.w = {eng: c}
            b.r = {}

    def dma(self, queue, sem, pairs, reads=(), writes=()):
        self._wait(queue, self._deps(reads, writes))
        q = self.engines[queue]
        for (o, i) in pairs:
            q.dma_start(out=o, in_=i).then_inc(self.sems[sem], 16)
            self.cnt[sem] += 16
        v = self.cnt[sem]
        for b in reads:
            b.r[sem] = v
        for b in writes:
            b.w = {sem: v}
            b.r = {}

    def final_wait(self, eng, sems):
        for s in sems:
            self.engines[eng].wait_ge(self.sems[s], self.cnt[s])


class _Stop(Exception):
    pass


def build_program(NL=NLAYERS, NSC=SEQ // 512, dbg=(), stop_after=99):
    L = NSC * 512
    NCH = L // 128
    nc = bass.Bass("TRN2", target_bir_lowering=False)
    dram = nc.dram_tensor
    x_d = dram("x", [L, DM], F32, kind="ExternalInput").ap()
    win_d = dram("w_in", [NL, 1024, 4368], F32, kind="ExternalInput").ap()
    wout_d = dram("w_out", [NL, 2048, 1024], F32, kind="ExternalInput").ap()
    wup_d = dram("w_up", [NL, 1024, 4096], F32, kind="ExternalInput").ap()
    wdn_d = dram("w_down", [NL, 4096, 1024], F32, kind="ExternalInput").ap()
    pcol_d = dram("pcol", [128, NL, NCOL], F32, kind="ExternalInput").ap()
    prow_d = dram("prow", [128, NL, NROW], F32, kind="ExternalInput").ap()
    cst_d = dram("cst", [128, 512], F32, kind="ExternalInput").ap()
    cos_d = dram("cosT", [128, L], F32, kind="ExternalInput").ap()
    sin_d = dram("sinT", [128, L], F32, kind="ExternalInput").ap()
    out_d = dram("out", [L, DM], F32, kind="ExternalOutput").ap()
    dbg_d = {}
    for (name, shape) in dbg:
        dbg_d[name] = dram("dbg_" + name, list(shape), F32, kind="ExternalOutput").ap()

    with ExitStack() as es:
        fw = FW(nc, es)

        def sb(name, shape, dt=F32):
            return es.enter_context(nc.sbuf_tensor("sb_" + name, list(shape), dt))

        def ps(name, shape, dt=F32):
            return es.enter_context(nc.psum_tensor("ps_" + name, list(shape), dt))

        xT = sb("xT", [128, 8, L]); bxT = [Buf("xT%d" % i) for i in range(NSC)]
        cst = sb("cst", [128, 512]); bcst = Buf("cst")
        ident_f, tri_f, gt_f, RT_f = cst[:, 0:128], cst[:, 128:256], cst[:, 256:384], cst[:, 384:512]
        cb16 = sb("cb16", [128, 5, 128], BF16); bcb16 = Buf("cb16")
        ident_b, tri_b, gt_b, ones_b, blk_b = (cb16[:, i, :] for i in range(5))
        mask2 = cb16[:, 1:3, :]
        ones_f = sb("ones_f", [128, 128]); bones_f = Buf("ones_f")
        pcol = sb("pcol", [128, NL, NCOL]); bpcol = Buf("pcol")
        prow = sb("prow", [128, NL, NROW]); bprow = Buf("prow")
        lay = sb("lay", [128, 32]); blay = Buf("lay")
        hT = sb("hT", [128, 8, 512], BF16); bhT = Buf("hT")
        tg1 = sb("tg1", [128, 512])
        stg = [sb("stg%d" % i, [128, 8, 512]) for i in range(2)]; bstg = [Buf("stg%d" % i) for i in range(2)]
        wbf = [sb("wbf%d" % i, [128, 8, 512], BF16) for i in range(2)]; bwbf = [Buf("wbf%d" % i) for i in range(2)]
        yT = sb("yT", [128, 16, 512], BF16); byT = [Buf("yT%d" % i) for i in range(4)]
        hst = sb("hst", [128, 2, 512]); bhst = [Buf("hst0"), Buf("hst1")]
        hstb = sb("hstb", [128, 2, 512], BF16); bhstb = [Buf("hstb0"), Buf("hstb1")]
        R = sb("R", [128, 20, 512]); bR = [Buf("R%d" % i) for i in range(20)]
        Rf = R
        Rb = R[:, :, :].bitcast(BF16)
        tg = [R[:, 19, :], tg1]; btg = [bR[19], Buf("tg1")]
        pc = [sb("pc%d" % i, [128, 515]) for i in range(2)]; bpc = [Buf("pc0"), Buf("pc1")]
        tails = sb("tails", [128, 12, 3]); btails = Buf("tails")
        hbuf = sb("hbuf", [128, 4, 542], BF16); bhbuf = Buf("hbuf")
        kT = sb("kT", [128, 2, 640], BF16); bkT = Buf("kT")
        Vaug = sb("Vaug", [128, 5, 2, 80], BF16); bV = Buf("Vaug")
        dtr = sb("dtr", [128, 4, 16]); bdtr = Buf("dtr")
        adt = sb("adt", [128, 4, 16]); badt = Buf("adt")
        est = sb("est", [128, 24]); best = Buf("est")
        w2 = sb("w2", [128, 8]); bw2 = Buf("w2")
        cbm = sb("cbm", [128, 128], BF16); bcbm = Buf("cbm")
        btok = sb("btok", [128, 128], BF16); bbtok = Buf("btok")
        den = sb("den", [128, 8]); bden = Buf("den")

        P = [ps("P%d" % i, [128, 1024]) for i in range(4)]
        bP = [Buf("bank%d" % i) for i in range(8)]

        def bank(i):
            return P[i // 2][:, (i % 2) * 512:(i % 2 + 1) * 512]

        s_misc = fw.new_dma_sem("s_misc")
        s_stg = [fw.new_dma_sem("s_stg0"), fw.new_dma_sem("s_stg1")]
        s_xin = [fw.new_dma_sem("s_xin0"), fw.new_dma_sem("s_xin1")]
        s_out = [fw.new_dma_sem("s_out0"), fw.new_dma_sem("s_out1")]
        s_cs = fw.new_dma_sem("s_cs")
        s_dbg = fw.new_dma_sem("s_dbg")

        def dump(name, ap, bufs):
            if name in dbg_d:
                fw.dma("sp", s_dbg, [(dbg_d[name], ap)], reads=bufs)

        fw.dma("sp", s_misc, [(cst[:], cst_d), (pcol[:], pcol_d), (prow[:], prow_d)], writes=[bcst, bpcol, bprow])
        fw.op("pool", lambda e: e.memset(ones_f[:], 1.0), writes=[bones_f])
        fw.op("pool", lambda e: e.memset(cb16[:, 3:5, :], 1.0), writes=[bcb16])
        fw.op("pool", lambda e: e.memset(cb16[0:64, 4, 64:128], 0.0), writes=[bcb16])
        fw.op("pool", lambda e: e.memset(cb16[64:128, 4, 0:64], 0.0), writes=[bcb16])
        fw.op("dve", lambda e: e.tensor_copy(cb16[:, 0:3, :], cst[:, 0:384].rearrange("p (a b) -> p a b", a=3)),
              reads=[bcst], writes=[bcb16])
        fw.op("pool", lambda e: e.memset(Vaug[:], 1.0), writes=[bV])

        for c in range(NCH):
            s = c % 2
            xin = Rf[:, 2 * s:2 * s + 2, :].rearrange("p a b -> p (a b)")
            bx = bR[2 * s:2 * s + 2]
            fw.dma("sp", s_xin[s], [(xin, x_d[c * 128:(c + 1) * 128, :])], writes=bx)
            pt = P[s]

            def f(pe, xin=xin, pt=pt):
                for k in range(8):
                    last = pe.transpose(pt[:, k * 128:(k + 1) * 128], xin[:, k * 128:(k + 1) * 128], ident_f)
                return last
            fw.op("pe", f, reads=bx + [bcst], writes=[bP[2 * s], bP[2 * s + 1]])
            eng = "act" if c % 2 == 0 else "dve"
            dst = xT[:, :, c * 128:(c + 1) * 128]
            src = pt[:, :].rearrange("p (k t) -> p k t", k=8)
            if eng == "act":
                fw.op("act", lambda e: e.activation(dst, src, AF.Copy), reads=[bP[2 * s], bP[2 * s + 1]], writes=[bxT[c // 4]])
            else:
                fw.op("dve", lambda e: e.tensor_copy(dst, src), reads=[bP[2 * s], bP[2 * s + 1]], writes=[bxT[c // 4]])

        pieces = []

        def add_piece(srcs, casts=None):
            if casts is None:
                n = sum(s[2] for s in srcs)
                casts = [(0, 0, n)]
            pieces.append((srcs, casts))

        for l in range(NL):
            Win = win_d[l].rearrange("(k p) n -> p k n", p=128)
            Wout = wout_d[l].rearrange("(k p) n -> p k n", p=128)
            Wup = wup_d[l].rearrange("(k p) n -> p k n", p=128)
            Wdn = wdn_d[l].rearrange("(k p) n -> p k n", p=128)
            for sc in range(NSC):
                add_piece([(0, Win[:, :, 3216:3344], 128), (128, Win[:, :, 2560:2576], 16)])
                for g in range(2):
                    add_piece([(0, Win[:, :, g * 512:(g + 1) * 512], 512)])
                    add_piece([(0, Win[:, :, 1024 + g * 512:1024 + (g + 1) * 512], 512)])
                    add_piece([(0, Win[:, :, 2048 + g * 128:2048 + (g + 1) * 128], 128),
                               (128, Win[:, :, 2304 + g * 128:2304 + (g + 1) * 128], 128)])
                add_piece([(0, Win[:, :, 2576:3088], 512)])
                add_piece([(0, Win[:, :, 3088:3216], 128)],
                          casts=[(0, 0, 64), (64, 0, 64), (128, 64, 64), (192, 64, 64)])
                add_piece([(0, Win[:, :, 3344:3856], 512)])
                add_piece([(0, Win[:, :, 3856:4368], 512)])
                for cb_ in range(2):
                    for kg in range(2):
                        add_piece([(0, Wout[:, kg * 8:(kg + 1) * 8, cb_ * 512:(cb_ + 1) * 512], 512)])
                for m in range(8):
                    add_piece([(0, Wup[:, :, m * 512:(m + 1) * 512], 512)])
                for cb_ in range(2):
                    for kg in range(4):
                        add_piece([(0, Wdn[:, kg * 8:(kg + 1) * 8, cb_ * 512:(cb_ + 1) * 512], 512)])
        NP = len(pieces)
        cast_engs = ["act", "act", "dve"]
        st = {"dma": 0, "cast": 0, "use": 0}

        def w_dma():
            i = st["dma"]
            if i >= NP:
                return
            s = i % 2
            srcs, _ = pieces[i]
            fw.dma("sp", s_stg[s], [(stg[s][:, :, c0:c0 + n], ap) for (c0, ap, n) in srcs], writes=[bstg[s]])
            st["dma"] += 1

        def w_cast():
            i = st["cast"]
            if i >= NP:
                return
            s = i % 2
            _, casts = pieces[i]
            eng = cast_engs[i % len(cast_engs)]
            for (b0, s0, n) in casts:
                o = wbf[s][:, :, b0:b0 + n]
                a = stg[s][:, :, s0:s0 + n]
                if eng == "act":
                    fw.op("act", lambda e: e.activation(o, a, AF.Copy), reads=[bstg[s]], writes=[bwbf[s]])
                else:
                    fw.op(eng, lambda e: e.tensor_copy(o, a), reads=[bstg[s]], writes=[bwbf[s]])
            st["cast"] += 1

        def w_acquire():
            i = st["use"]
            return wbf[i % 2], bwbf[i % 2]

        def w_release():
            st["use"] += 1
            w_cast()
            w_dma()

        w_dma(); w_dma(); w_cast(); w_dma(); w_cast(); w_dma()

        acc_rr = [0]

        def stage(n):
            if n >= stop_after:
                raise _Stop()

        def next_acc():
            i = acc_rr[0] % 4
            acc_rr[0] += 1
            return i

        def mm_feat(bi, wt, bw, j, nk=8):
            def f(pe):
                for k in range(nk):
                    last = pe.matmul(bank(bi), wt[:, k, j * 128:(j + 1) * 128], hT[:, k, :], start=(k == 0), stop=(k == nk - 1))
                return last
            fw.op("pe", f, reads=[bw, bhT], writes=[bP[bi]])

        def rms_norm_to_hT(sc, wofs, l):
            tok = slice(sc * 512, (sc + 1) * 512)
            sq = yT[:, 0:8, :]
            fw.op("act", lambda e: e.activation(sq, xT[:, :, tok], AF.Square), reads=[bxT[sc]], writes=[byT[0], byT[1]])

            def f(pe):
                for k in range(8):
                    last = pe.matmul(bank(4), ones_b, yT[:, k, :], start=(k == 0), stop=(k == 7))
                return last
            fw.op("pe", f, reads=[byT[0], byT[1], bcb16], writes=[bP[4]])
            fw.op("act", lambda e: e.activation(tg[0], bank(4), AF.Sqrt, bias=RMS_EPS, scale=1.0 / DM), reads=[bP[4]], writes=[btg[0]])
            fw.op("dve", lambda e: e.reciprocal(tg[0], tg[0]), reads=[btg[0]], writes=[btg[0]])
            for k in range(8):
                eng = "dve"
                fw.op(eng, lambda e: e.scalar_tensor_tensor(out=hT[:, k, :], in0=xT[:, k, tok], scalar=pcol[:, l, wofs + k:wofs + k + 1],
                                                            in1=tg[0], op0=ALU.mult, op1=ALU.mult),
                      reads=[bxT[sc], btg[0], bpcol], writes=[bhT])

        try:
          for l in range(NL):
            fw.op("act", lambda e: e.activation(lay[:, 0:16], prow[:, l, ALOG:ALOG + 16], AF.Exp), reads=[bprow], writes=[blay])
            fw.op("dve", lambda e: e.tensor_scalar(lay[:, 0:16], lay[:, 0:16], -1.0, None, ALU.mult), reads=[blay], writes=[blay])
            fw.op("act", lambda e: e.activation(lay[:, 16:24], prow[:, l, SINK:SINK + 8], AF.Exp), reads=[bprow], writes=[blay])
            fw.op("pool", lambda e: e.memset(hst[:], 0.0), writes=bhst)
            fw.op("pool", lambda e: e.memset(hstb[:], 0.0), writes=bhstb)

            for sc in range(NSC):
                tok = slice(sc * 512, (sc + 1) * 512)
                first = (sc == 0)
                rms_norm_to_hT(sc, NMW, l)

                wt, bw = w_acquire()
                for c in range(4):
                    cs = slice(c * 128, (c + 1) * 128)

                    def f(pe):
                        for k in range(8):
                            last = pe.matmul(bank(4)[:, 0:144], hT[:, k, cs], wt[:, k, 0:144], start=(k == 0), stop=(k == 7))
                        return last
                    fw.op("pe", f, reads=[bw, bhT], writes=[bP[4]])
                    fw.op("act", lambda e: e.activation(Vaug[:, 1 + c, :, 0:64], bank(4)[:, 0:128].rearrange("p (g d) -> p g d", g=2), AF.Copy),
                          reads=[bP[4]], writes=[bV])
                    fw.op("dve", lambda e: e.tensor_tensor(out=dtr[:, c, :], in0=bank(4)[:, 128:144], in1=prow[:, l, DTB:DTB + 16], op=ALU.add),
                          reads=[bP[4], bprow], writes=[bdtr])
                w_release()
                fw.op("act", lambda e: e.activation(dtr[:], dtr[:], AF.Exp), reads=[bdtr], writes=[bdtr])
                fw.op("act", lambda e: e.activation(dtr[:], dtr[:], AF.Ln, bias=1.0, scale=1.0), reads=[bdtr], writes=[bdtr])
                fw.op("pool", lambda e: e.tensor_tensor(out=adt[:], in0=dtr[:], in1=lay[:, 0:16].unsqueeze(1).broadcast_to([128, 4, 16]), op=ALU.mult),
                      reads=[bdtr, blay], writes=[badt])

                stage(1)
                zsT = Rf[:, 0:4, :]; bz = bR[0:4]
                xsT = Rf[:, 4:8, :]; bxs = bR[4:8]
                BT = Rb[:, 8, 0:512]; CT = Rb[:, 8, 512:1024]; bBC = bR[8]
                accb = [Rf[:, 9, :], Rf[:, 10, :]]; baccb = [bR[9], bR[10]]
                rseg = Rf[:, 11:13, :].rearrange("p a b -> p (a b)").rearrange("p (h l) -> p h l", h=8); brseg = bR[11:13]
                dec = Rb[:, 13, :]; dec3 = dec.rearrange("p (h l) -> p h l", h=8); bdec = bR[13]
                xdt = Rb[:, 14, 0:512]; xdtd = Rb[:, 14, 512:1024]; bxdt = bR[14]
                xsD = Rf[:, 15, :]; bxsD = bR[15]
                t1 = Rf[:, 16, :]; bt1 = bR[16]
                ytok = Rf[:, 17, :]; bytok = bR[17]
                sqg = Rb[:, 18:20, :].rearrange("p a b -> p (a b)").rearrange("p (j t) -> p j t", j=4); bsqg = bR[18:20]
                conv_i = [0]

                def conv_chunk(bi, cj, dest, bdest, dest_is_list=False):
                    i = conv_i[0] % 2
                    conv_i[0] += 1
                    p, bp = pc[i], bpc[i]
                    a, ba = accb[i], baccb[i]
                    if first:
                        fw.op("pool", lambda e: e.memset(p[:, 0:3], 0.0), writes=[bp])
                    else:
                        fw.op("pool", lambda e: e.tensor_copy(p[:, 0:3], tails[:, cj, :]), reads=[btails], writes=[bp])
                    fw.op("act", lambda e: e.activation(p[:, 3:515], bank(bi), AF.Copy), reads=[bP[bi]], writes=[bp])
                    fw.op("pool", lambda e: e.tensor_copy(tails[:, cj, :], p[:, 512:515]), reads=[bp], writes=[btails])
                    co = CW + cj * 4
                    fw.op("act", lambda e: e.activation(a, p[:, 0:512], AF.Identity, bias=pcol[:, l, CB + cj:CB + cj + 1], scale=pcol[:, l, co:co + 1]),
                          reads=[bp, bpcol], writes=[ba])
                    for k in range(1, 4):
                        eng = "dve"
                        fw.op(eng, lambda e: e.scalar_tensor_tensor(out=a, in0=p[:, k:k + 512], scalar=pcol[:, l, co + k:co + k + 1], in1=a,
                                                                    op0=ALU.mult, op1=ALU.add),
                              reads=[bp, bpcol, ba], writes=[ba])
                    fw.op("act", lambda e: e.activation(dest, a, AF.Silu), reads=[ba], writes=bdest)

                for g in range(2):
                    hs = slice(g * 8, (g + 1) * 8)
                    wt, bw = w_acquire()
                    for j in range(4):
                        bi = next_acc()
                        mm_feat(bi, wt, bw, j)
                        fw.op("act", lambda e: e.activation(zsT[:, j, :], bank(bi), AF.Silu), reads=[bP[bi]], writes=[bz[j]])
                    w_release()
                    wt, bw = w_acquire()
                    for j in range(4):
                        bi = next_acc()
                        mm_feat(bi, wt, bw, j)
                        conv_chunk(bi, g * 4 + j, xsT[:, j, :], [bxs[j]])
                    w_release()
                    wt, bw = w_acquire()
                    for j in range(2):
                        bi = next_acc()
                        mm_feat(bi, wt, bw, j)
                        conv_chunk(bi, 8 + 2 * j + g, BT if j == 0 else CT, [bBC])
                    w_release()

                    stage(2 if g == 0 else 3.5)
                    for c in range(4):
                        cs = slice(c * 128, (c + 1) * 128)
                        adt_c = adt[:, c, hs]
                        dt_c = dtr[:, c, hs]
                        hg = hst[:, g, :]
                        xs3 = bank(5).rearrange("p (h d) -> p h d", h=8)

                        def f(pe):
                            pe.matmul(bank(4)[:, 0:8], tri_f, adt_c, start=True, stop=True)
                            pe.matmul(bank(4)[:, 8:16], gt_f, adt_c, start=True, stop=True)
                            pe.matmul(bank(4)[:, 16:24], ones_f[:], adt_c, start=True, stop=True)
                            return pe.matmul(bank(4)[:, 192:320], BT[:, cs], ident_b, start=True, stop=True)
                        fw.op("pe", f, reads=[badt, bcst, bones_f, bBC, bcb16], writes=[bP[4]])
                        fw.op("pe", lambda pe: pe.matmul(bank(2)[:, 0:128], BT[:, cs], CT[:, cs], start=True, stop=True), reads=[bBC], writes=[bP[2]])

                        def f(pe):
                            for j in range(4):
                                last = pe.transpose(bank(5)[:, j * 128:(j + 1) * 128], xsT[:, j, cs], ident_f)
                            return last
                        fw.op("pe", f, reads=bxs + [bcst], writes=[bP[5]])
                        fw.op("pool", lambda e: e.tensor_tensor(out=rseg, in0=tri_f.unsqueeze(1).broadcast_to([128, 8, 128]),
                                                                 in1=adt_c.unsqueeze(2).broadcast_to([128, 8, 128]), op=ALU.mult),
                              reads=[bcst, badt], writes=brseg)
                        fw.op("act", lambda e: e.activation(est[:], bank(4)[:, 0:24], AF.Exp), reads=[bP[4]], writes=[best])
                        fw.op("act", lambda e: e.activation(btok[:], bank(4)[:, 192:320], AF.Copy), reads=[bP[4]], writes=[bbtok])
                        fw.op("dve", lambda e: e.tensor_tensor(out=cbm[:], in0=bank(2)[:, 0:128], in1=tri_f, op=ALU.mult),
                              reads=[bP[2], bcst], writes=[bcbm])
                        fw.op("dve", lambda e: e.tensor_tensor(out=xdt.rearrange("p (h d) -> p h d", h=8), in0=xs3,
                                                                in1=dt_c.unsqueeze(2).broadcast_to([128, 8, 64]), op=ALU.mult),
                              reads=[bP[5], bdtr], writes=[bxdt])
                        fw.op("pool", lambda e: e.tensor_tensor(out=w2[:], in0=dt_c, in1=est[:, 8:16], op=ALU.mult), reads=[bdtr, best], writes=[bw2])

                        def f(pe):
                            pe.matmul(P[3][:, 0:512], gt_f, rseg[:, 0:4, :], start=True, stop=True)
                            return pe.matmul(P[3][:, 512:1024], gt_f, rseg[:, 4:8, :], start=True, stop=True)
                        fw.op("pe", f, reads=brseg + [bcst], writes=[bP[6], bP[7]])
                        fw.op("act", lambda e: e.activation(dec, P[3][:, :], AF.Exp), reads=[bP[6], bP[7]], writes=[bdec])
                        fw.op("dve", lambda e: e.tensor_tensor(out=xdtd.rearrange("p (h d) -> p h d", h=8), in0=xs3,
                                                                in1=w2[:].unsqueeze(2).broadcast_to([128, 8, 64]), op=ALU.mult),
                              reads=[bP[5], bw2], writes=[bxdt])
                        fw.op("dve", lambda e: e.tensor_tensor(out=xsD.rearrange("p (h d) -> p h d", h=8), in0=xs3,
                                                                in1=prow[:, l, DROW + g * 8:DROW + (g + 1) * 8].unsqueeze(2).broadcast_to([128, 8, 64]), op=ALU.mult),
                              reads=[bP[5], bprow], writes=[bxsD])
                        fw.op("dve", lambda e: e.tensor_tensor(out=dec3, in0=dec3, in1=cbm[:].unsqueeze(1).broadcast_to([128, 8, 128]), op=ALU.mult),
                              reads=[bdec, bcbm], writes=[bdec])
                        fw.op("pe", lambda pe: pe.matmul(bank(1), CT[:, cs], hstb[:, g, :], start=True, stop=True), reads=[bBC, bhstb[g]], writes=[bP[1]])
                        fw.op("pe", lambda pe: pe.matmul(bank(2), btok[:], xdtd, start=True, stop=True), reads=[bbtok, bxdt], writes=[bP[2]])
                        fw.op("pool", lambda e: e.tensor_tensor(out=hg.rearrange("p (h d) -> p h d", h=8), in0=hg.rearrange("p (h d) -> p h d", h=8),
                                                                 in1=est[:, 16:24].unsqueeze(2).broadcast_to([128, 8, 64]), op=ALU.mult),
                              reads=[bhst[g], best], writes=[bhst[g]])
                        fw.op("dve", lambda e: e.tensor_tensor(out=t1.rearrange("p (h d) -> p h d", h=8), in0=bank(1).rearrange("p (h d) -> p h d", h=8),
                                                                in1=est[:, 0:8].unsqueeze(2).broadcast_to([128, 8, 64]), op=ALU.mult),
                              reads=[bP[1], best], writes=[bt1])
                        fw.op("pool", lambda e: e.tensor_tensor(out=t1, in0=t1, in1=xsD, op=ALU.add), reads=[bt1, bxsD], writes=[bt1])
                        fw.op("dve", lambda e: e.tensor_tensor(out=hg, in0=bank(2), in1=hg, op=ALU.add), reads=[bP[2], bhst[g]], writes=[bhst[g]])
                        fw.op("act", lambda e: e.activation(hstb[:, g, :], hg, AF.Copy), reads=[bhst[g]], writes=[bhstb[g]])

                        def f(pe):
                            for h in range(8):
                                last = pe.matmul(bank(0)[:, h * 64:(h + 1) * 64], dec3[:, h, :], xdt[:, h * 64:(h + 1) * 64], start=True, stop=True)
                            return last
                        fw.op("pe", f, reads=[bdec, bxdt], writes=[bP[0]])
                        fw.op("dve", lambda e: e.tensor_tensor(out=ytok, in0=bank(0), in1=t1, op=ALU.add), reads=[bP[0], bt1], writes=[bytok])

                        def f(pe):
                            for j in range(4):
                                last = pe.transpose(bank(3)[:, j * 128:(j + 1) * 128], ytok[:, j * 128:(j + 1) * 128], ident_f)
                            return last
                        fw.op("pe", f, reads=[bytok, bcst], writes=[bP[3]])
                        fw.op("dve", lambda e: e.tensor_tensor(out=zsT[:, :, cs], in0=bank(3).rearrange("p (j t) -> p j t", j=4), in1=zsT[:, :, cs], op=ALU.mult),
                              reads=[bP[3]] + bz, writes=bz)

                    stage(2.5 if g == 0 else 3.7)
                    fw.op("act", lambda e: e.activation(sqg, zsT, AF.Square), reads=bz, writes=bsqg)

                    def f(pe):
                        for j in range(4):
                            last = pe.matmul(bank(5), ones_b, sqg[:, j, :], start=(j == 0), stop=(j == 3))
                        return last
                    fw.op("pe", f, reads=bsqg + [bcb16], writes=[bP[5]])
                    fw.op("act", lambda e: e.activation(tg[1][:, :], bank(5), AF.Sqrt, bias=RMS_EPS, scale=1.0 / 512), reads=[bP[5]], writes=[btg[1]])
                    fw.op("dve", lambda e: e.reciprocal(tg[1][:, :], tg[1][:, :]), reads=[btg[1]], writes=[btg[1]])
                    for j in range(4):
                        eng = "dve"
                        so = SNW + g * 4 + j
                        fw.op(eng, lambda e: e.scalar_tensor_tensor(out=yT[:, g * 4 + j, :], in0=zsT[:, j, :], scalar=pcol[:, l, so:so + 1], in1=tg[1][:, :],
                                                                    op0=ALU.mult, op1=ALU.mult),
                              reads=[bz[j], btg[1], bpcol], writes=[byT[g]])

                stage(4)
                qTb = Rb[:, 0:2, :]
                bq = bR[0:2]
                qraw = Rf[:, 2, :]; qn = Rf[:, 3, :]; ta = Rf[:, 4, :]; tb = Rf[:, 5, :]
                cosS = Rf[:, 6, :]; sinS = Rf[:, 7, :]
                sqq = Rb[:, 8, 0:512]
                rsq = Rf[:, 9, :]
                PT = [Rb[:, 10, :], Rb[:, 11, :]]
                yat = Rf[:, 16, :]
                fw.dma("sp", s_cs, [(cosS, cos_d[:, tok]), (sinS, sin_d[:, tok])], writes=[bR[6], bR[7]])
                qhi = Rb[:, 14:16, :]
                bqh = bR[14:16]
                fw.op("pool", lambda e: e.memset(qTb, 0.0), writes=bq)
                fw.op("pool", lambda e: e.memset(qhi, 0.0), writes=bqh)
                qraw_s = [Rf[:, 2, :], Rf[:, 16, :]]; bqraw = [bR[2], bR[16]]
                qn_s = [Rf[:, 3, :], Rf[:, 17, :]]; bqn = [bR[3], bR[17]]
                ta_s = [Rf[:, 4, :], Rf[:, 18, :]]; bta = [bR[4], bR[18]]
                tb_s = [Rf[:, 5, :], Rf[:, 19, :]]; btb = [bR[5], bR[19]]
                sqq_s = [Rb[:, 8, 0:512], Rb[:, 12, 0:512]]; bsqq = [bR[8], bR[12]]
                rsq_s = [Rf[:, 9, :], Rf[:, 13, :]]; brsq = [bR[9], bR[13]]
                pb1 = [5, 4]; pb2 = [6, 7]
                acc_of = {}
                wts = {}

                def qk_s1(i):
                    if i == 0 or i == 4:
                        wts["cur"] = w_acquire()
                    wt, bw = wts["cur"]
                    u = i % 2
                    j = i if i < 4 else i - 4
                    bi = next_acc()
                    acc_of[i] = bi
                    mm_feat(bi, wt, bw, j)
                    fw.op("act", lambda e: e.activation(qraw_s[u], bank(bi), AF.Copy), reads=[bP[bi]], writes=[bqraw[u]])
                    fw.op("act", lambda e: e.activation(sqq_s[u], bank(bi), AF.Square), reads=[bP[bi]], writes=[bsqq[u]])
                    if i == 3 or i == 5:
                        w_release()

                def qk_s2(i):
                    u = i % 2
                    fw.op("pe", lambda pe: pe.matmul(bank(pb1[u]), blk_b, sqq_s[u], start=True, stop=True), reads=[bsqq[u], bcb16], writes=[bP[pb1[u]]])
                    fw.op("act", lambda e: e.activation(rsq_s[u], bank(pb1[u]), AF.Sqrt, bias=RMS_EPS, scale=1.0 / 64), reads=[bP[pb1[u]]], writes=[brsq[u]])
                    fw.op("dve", lambda e: e.reciprocal(rsq_s[u], rsq_s[u]), reads=[brsq[u]], writes=[brsq[u]])
                    wo = QW if i < 4 else KW
                    fw.op("dve", lambda e: e.scalar_tensor_tensor(out=qn_s[u], in0=qraw_s[u], scalar=pcol[:, l, wo:wo + 1], in1=rsq_s[u], op0=ALU.mult, op1=ALU.mult),
                          reads=[bqraw[u], brsq[u], bpcol], writes=[bqn[u]])

                def qk_s3(i):
                    u = i % 2
                    ta, tb = ta_s[u], tb_s[u]
                    fw.op("pe", lambda pe: pe.matmul(bank(pb2[u]), RT_f, qn_s[u], start=True, stop=True), reads=[bqn[u], bcst], writes=[bP[pb2[u]]])
                    fw.op("pool", lambda e: e.tensor_tensor(out=ta, in0=qn_s[u], in1=cosS, op=ALU.mult), reads=[bqn[u], bR[6]], writes=[bta[u]])
                    fw.op("dve", lambda e: e.tensor_tensor(out=tb, in0=bank(pb2[u]), in1=sinS, op=ALU.mult), reads=[bP[pb2[u]], bR[7]], writes=[btb[u]])
                    if i < 4:
                        cc = slice((i % 2) * 512, (i % 2 + 1) * 512)
                        fw.op("pool", lambda e: e.tensor_tensor(out=qTb[0:64, i // 2, cc], in0=ta[0:64, :], in1=tb[0:64, :], op=ALU.add),
                              reads=[bta[u], btb[u]], writes=[bq[i // 2]])
                        fw.op("dve", lambda e: e.tensor_tensor(out=qhi[64:128, i // 2, cc], in0=ta[64:128, :], in1=tb[64:128, :], op=ALU.add),
                              reads=[bta[u], btb[u]], writes=[bqh[i // 2]])
                    else:
                        dst = kT[:, i - 4, 128:640]
                        fw.op("pool", lambda e: e.tensor_tensor(out=dst, in0=ta, in1=tb, op=ALU.add), reads=[bta[u], btb[u]], writes=[bkT])

                qk_s1(0)
                for i in range(6):
                    if i + 1 < 6:
                        qk_s1(i + 1)
                    qk_s2(i)
                    qk_s3(i)

                stage(5)
                for c in range(4):
                    cs = slice(c * 128, (c + 1) * 128)
                    gc = sc * 4 + c
                    kcs = [1] if gc == 0 else [0, 1]
                    for g in range(2):
                        PS = P[3][:, :].rearrange("p (kc jj par q) -> p kc jj par q", kc=2, jj=2, par=2)

                        def f(pe):
                            for kc in kcs:
                                k0 = c * 128 + (128 if kc == 1 else 0)
                                for par in range(2):
                                    qsrc = qTb if par == 0 else qhi
                                    for jj in range(2):
                                        rhs = qsrc[:, g, jj * 512 + c * 128:jj * 512 + (c + 1) * 128]
                                        last = pe.matmul(PS[:, kc, jj, par, :], kT[:, g, k0:k0 + 128], rhs, start=True, stop=True)
                            return last
                        fw.op("pe", f, reads=bq + bqh + [bkT], writes=[bP[6], bP[7]])
                        pt4 = PT[g].rearrange("p (kc r q) -> p kc r q", kc=2, r=4)
                        ps4 = P[3][:, :].rearrange("p (kc r q) -> p kc r q", kc=2, r=4)
                        k_lo = kcs[0]
                        fw.op("act", lambda e: e.activation(PT[g][:, k_lo * 512:1024], P[3][:, k_lo * 512:1024], AF.Exp, scale=0.125),
                              reads=[bP[6], bP[7]], writes=[bR[10 + g]])
                        fw.op("dve", lambda e: e.tensor_tensor(out=pt4[:, 1], in0=pt4[:, 1], in1=tri_b.unsqueeze(1).broadcast_to([128, 4, 128]), op=ALU.mult),
                              reads=[bR[10 + g], bcb16], writes=[bR[10 + g]])
                        if gc > 0:
                            fw.op("pool", lambda e: e.tensor_tensor(out=pt4[:, 0], in0=pt4[:, 0], in1=gt_b.unsqueeze(1).broadcast_to([128, 4, 128]), op=ALU.mult),
                                  reads=[bR[10 + g], bcb16], writes=[bR[10 + g]])
                        ob = g
                        O = bank(ob).rearrange("p (r e) -> p r e", r=4)

                        def f(pe):
                            for r in range(4):
                                jj, par = r // 2, r % 2
                                for kc in kcs:
                                    last = pe.matmul(O[:, r, 0:72], pt4[:, kc, jj * 2 + par, :], Vaug[:, c + kc, g, 0:72],
                                                     start=(kc == kcs[0]), stop=(kc == 1))
                            return last
                        fw.op("pe", f, reads=[bR[10 + g], bV], writes=[bP[ob]])
                        fw.op("dve", lambda e: e.tensor_tensor(out=den[:, g * 4:(g + 1) * 4], in0=O[:, :, 64], in1=lay[:, 16 + g * 4:16 + (g + 1) * 4], op=ALU.add),
                              reads=[bP[ob], blay], writes=[bden])
                        fw.op("dve", lambda e: e.reciprocal(den[:, g * 4:(g + 1) * 4], den[:, g * 4:(g + 1) * 4]), reads=[bden], writes=[bden])
                        fw.op("dve", lambda e: e.tensor_tensor(out=yat.rearrange("p (h d) -> p h d", h=8)[:, g * 4:(g + 1) * 4, :], in0=O[:, :, 0:64],
                                                                in1=den[:, g * 4:(g + 1) * 4].unsqueeze(2).broadcast_to([128, 4, 64]), op=ALU.mult),
                              reads=[bP[ob], bden], writes=[bR[16]])

                    def f(pe):
                        for j in range(4):
                            last = pe.transpose(bank(2)[:, j * 128:(j + 1) * 128], yat[:, j * 128:(j + 1) * 128], ident_f)
                        return last
                    fw.op("pe", f, reads=[bR[16], bcst], writes=[bP[2]])
                    fw.op("act", lambda e: e.activation(yT[:, 8:12, cs], bank(2).rearrange("p (j t) -> p j t", j=4), AF.Copy), reads=[bP[2]], writes=[byT[2]])
                fw.op("pool", lambda e: e.tensor_copy(kT[:, :, 0:128], kT[:, :, 512:640]), reads=[bkT], writes=[bkT])
                fw.op("pool", lambda e: e.tensor_copy(Vaug[:, 0], Vaug[:, 4]), reads=[bV], writes=[bV])

                stage(6)
                aT = Rf[:, 0:4, :]; ba = bR[0:4]
                cT = Rf[:, 4:8, :]; bc = bR[4:8]
                sig = Rf[:, 8, :]
                dgs = Rb[:, 9:13, :].rearrange("p a b -> p (a b)")[:, 0:3968].rearrange("p (k c) -> p k c", c=128); bdg = bR[9:13]
                sqc = Rb[:, 13:15, :].rearrange("p a b -> p (a b)").rearrange("p (j t) -> p j t", j=4); bsqc = bR[13:15]
                mean = Rf[:, 17, :]; rstd = Rf[:, 18, :]; msq = Rf[:, 19, :]
                dgs2 = hT[:, :, :].rearrange("p a b -> p (a b)")[:, 0:3968].rearrange("p (k c) -> p k c", c=128)
                dg_t = [dgs, dgs2]
                dg_b = [bdg, [bhT]]

                def build_dgs(j):
                    wofs = DWW + j * 31
                    fw.op("pool", lambda e: e.tensor_tensor(out=dg_t[j % 2], in0=ident_f.unsqueeze(1).broadcast_to([128, 31, 128]),
                                                             in1=pcol[:, l, wofs:wofs + 31].unsqueeze(2).broadcast_to([128, 31, 128]), op=ALU.mult),
                          reads=[bcst, bpcol], writes=dg_b[j % 2])
                build_dgs(0)
                if first:
                    fw.op("pool", lambda e: e.memset(hbuf[:, :, 0:30], 0.0), writes=[bhbuf])
                else:
                    fw.op("pool", lambda e: e.tensor_copy(hbuf[:, :, 0:30], hbuf[:, :, 512:542]), reads=[bhbuf], writes=[bhbuf])
                wt, bw = w_acquire()
                for j in range(4):
                    bi = next_acc()
                    mm_feat(bi, wt, bw, j)
                    fw.op("act", lambda e: e.activation(aT[:, j, :], bank(bi), AF.Copy), reads=[bP[bi]], writes=[ba[j]])
                w_release()
                wt, bw = w_acquire()
                for j in range(4):
                    bi = next_acc()
                    mm_feat(bi, wt, bw, j)
                    fw.op("act", lambda e: e.activation(sig, bank(bi), AF.Sigmoid), reads=[bP[bi]], writes=[bR[8]])
                    fw.op("dve", lambda e: e.tensor_tensor(out=hbuf[:, j, 30:542], in0=aT[:, j, :], in1=sig, op=ALU.mult), reads=[ba[j], bR[8]], writes=[bhbuf])
                w_release()
                build_dgs(1)
                for j in range(4):
                    bi = next_acc()
                    dgj = dg_t[j % 2]

                    def f(pe):
                        for k in range(31):
                            last = pe.matmul(bank(bi), dgj[:, k, :], hbuf[:, j, k:k + 512], start=(k == 0), stop=(k == 30))
                        return last
                    fw.op("pe", f, reads=dg_b[j % 2] + [bhbuf], writes=[bP[bi]])
                    if j + 2 < 4:
                        build_dgs(j + 2)
                    bcol = pcol[:, l, DWB + j:DWB + j + 1]
                    fw.op("act", lambda e: e.activation(cT[:, j, :], bank(bi), AF.Identity, bias=bcol, scale=1.0), reads=[bP[bi], bpcol], writes=[bc[j]])
                    fw.op("act", lambda e: e.activation(sqc[:, j, :], bank(bi), AF.Square, bias=bcol, scale=1.0), reads=[bP[bi], bpcol], writes=bsqc)

                def f(pe):
                    for j in range(4):
                        pe.matmul(bank(5), ones_f[:], cT[:, j, :], start=(j == 0), stop=(j == 3))
                    for j in range(4):
                        last = pe.matmul(bank(6), ones_b, sqc[:, j, :], start=(j == 0), stop=(j == 3))
                    return last
                fw.op("pe", f, reads=bc + bsqc + [bones_f, bcb16], writes=[bP[5], bP[6]])
                fw.op("dve", lambda e: e.tensor_scalar(mean, bank(5), 1.0 / 512, None, ALU.mult), reads=[bP[5]], writes=[bR[17]])
                fw.op("pool", lambda e: e.tensor_tensor(out=msq, in0=mean, in1=mean, op=ALU.mult), reads=[bR[17]], writes=[bR[19]])
                fw.op("dve", lambda e: e.scalar_tensor_tensor(out=rstd, in0=bank(6), scalar=1.0 / 512, in1=msq, op0=ALU.mult, op1=ALU.subtract),
                      reads=[bP[6], bR[19]], writes=[bR[18]])
                fw.op("act", lambda e: e.activation(rstd, rstd, AF.Sqrt, bias=LN_EPS, scale=1.0), reads=[bR[18]], writes=[bR[18]])
                fw.op("dve", lambda e: e.reciprocal(rstd, rstd), reads=[bR[18]], writes=[bR[18]])
                for j in range(4):
                    fw.op("pool", lambda e: e.tensor_tensor(out=cT[:, j, :], in0=cT[:, j, :], in1=mean, op=ALU.subtract), reads=[bc[j], bR[17]], writes=[bc[j]])
                    fw.op("dve", lambda e: e.tensor_tensor(out=cT[:, j, :], in0=cT[:, j, :], in1=rstd, op=ALU.mult), reads=[bc[j], bR[18]], writes=[bc[j]])
                    fw.op("act", lambda e: e.activation(yT[:, 12 + j, :], cT[:, j, :], AF.Silu, bias=pcol[:, l, LNB + j:LNB + j + 1], scale=pcol[:, l, LNW + j:LNW + j + 1]),
                          reads=[bc[j], bpcol], writes=[byT[3]])

                if "yT" in dbg_d and l == 0 and sc == 0:
                    fw.op("act", lambda e: e.activation(Rf[:, 0:16, :], yT[:, :, :], AF.Copy), reads=byT, writes=bR[0:16])
                    dump("yT", Rf[:, 0:16, :], bR[0:16])

                stage(7)
                def proj_residual(nkg, rhs_of):
                    for cb_ in range(2):
                        for kg in range(nkg):
                            wt, bw = w_acquire()

                            def f(pe):
                                for j in range(4):
                                    for kk in range(8):
                                        last = pe.matmul(bank(j), wt[:, kk, j * 128:(j + 1) * 128], rhs_of(kg * 8 + kk),
                                                         start=(kg == 0 and kk == 0), stop=(kg == nkg - 1 and kk == 7))
                                return last
                            fw.op("pe", f, reads=[bw] + rhs_bufs, writes=bP[0:4])
                            w_release()
                        for j in range(4):
                            xs_ = xT[:, cb_ * 4 + j, tok]
                            fw.op("dve", lambda e: e.tensor_tensor(out=xs_, in0=bank(j), in1=xs_, op=ALU.add), reads=[bP[j], bxT[sc]], writes=[bxT[sc]])

                rhs_bufs = byT
                proj_residual(2, lambda m: yT[:, m, :])

                stage(8)
                rms_norm_to_hT(sc, NLW, l)
                actb = Rb[:, 0:16, :]

                def act_chunk(m):
                    return actb[:, m // 2, (m % 2) * 512:(m % 2 + 1) * 512]
                for m in range(8):
                    wt, bw = w_acquire()
                    for j in range(4):
                        bi = next_acc()
                        mm_feat(bi, wt, bw, j)
                        ch = m * 4 + j
                        tmp, btmp = (Rf[:, 16 + (j % 2), :], bR[16 + (j % 2)])
                        fw.op("act", lambda e: e.activation(tmp, bank(bi), AF.Relu), reads=[bP[bi]], writes=[btmp])
                        eng = "dve" if j % 2 == 0 else "pool"
                        fw.op(eng, lambda e: e.tensor_tensor(out=act_chunk(ch), in0=tmp, in1=tmp, op=ALU.mult), reads=[btmp], writes=[bR[ch // 2]])
                    w_release()
                rhs_bufs = bR[0:16]
                proj_residual(4, act_chunk)

                if "x_after" in dbg_d and l == 0 and sc == 0:
                    dump("x_after", xT[:, :, 0:512], [bxT[0]])

        except _Stop:
            pass
        for c in range(NCH):
            s = c % 2
            pt = P[s]

            def f(pe):
                for k in range(8):
                    last = pe.transpose(pt[:, k * 128:(k + 1) * 128], xT[:, k, c * 128:(c + 1) * 128], ident_f)
                return last
            fw.op("pe", f, reads=[bxT[c // 4], bcst], writes=[bP[2 * s], bP[2 * s + 1]])
            xo = Rf[:, 2 * s:2 * s + 2, :].rearrange("p a b -> p (a b)")
            bx = bR[2 * s:2 * s + 2]
            if c % 2 == 0:
                fw.op("act", lambda e: e.activation(xo, pt[:, :], AF.Copy), reads=[bP[2 * s], bP[2 * s + 1]], writes=bx)
            else:
                fw.op("dve", lambda e: e.tensor_copy(xo, pt[:, :]), reads=[bP[2 * s], bP[2 * s + 1]], writes=bx)
            fw.dma("sp", s_out[s], [(out_d[c * 128:(c + 1) * 128, :], xo)], reads=bx)
        fw.final_wait("sp", [s_out[0], s_out[1], s_dbg, s_stg[0], s_stg[1], s_cs, s_misc, s_xin[0], s_xin[1]])
        print("program: ops=%d waits=%d pieces=%d sbuf_left=%d" % (fw.nops, fw.nwaits, NP, nc.sbuf_bytes_remaining))
    return nc


def host_consts(L):
    cst = np.zeros((128, 512), np.float32)
    i = np.arange(128)
    cst[:, 0:128] = np.eye(128, dtype=np.float32)
    cst[:, 128:256] = (i[:, None] <= i[None, :])
    cst[:, 256:384] = (i[:, None] > i[None, :])
    RT = np.zeros((128, 128), np.float32)
    for blk in (0, 64):
        for d in range(32):
            RT[blk + d + 32, blk + d] = -1.0
            RT[blk + d, blk + d + 32] = 1.0
    cst[:, 384:512] = RT
    inv_freq = (10000.0 ** (-np.arange(0, 64, 2, dtype=np.float32) / 64)).astype(np.float32)
    ang = np.arange(L, dtype=np.float32)[:, None] * inv_freq[None, :]
    cos = np.cos(ang).astype(np.float32).T
    sin = np.sin(ang).astype(np.float32).T
    cosT = np.ascontiguousarray(np.tile(cos, (4, 1)))
    sinT = np.ascontiguousarray(np.tile(sin, (4, 1)))
    return cst, cosT, sinT


def host_params(NL, norm_mix_w, ssd_conv_w, ssd_conv_b, ssd_dt_bias, ssd_a_log, ssd_d, ssd_norm_w, q_norm_w, k_norm_w,
                attn_sinks, cm_dw_w, cm_dw_b, cm_ln_w, cm_ln_b, norm_mlp_w):
    pcol = np.zeros((128, NL, NCOL), np.float32)
    prow = np.zeros((128, NL, NROW), np.float32)
    for l in range(NL):
        pcol[:, l, NMW:NMW + 8] = norm_mix_w[l].reshape(8, 128).T
        pcol[:, l, NLW:NLW + 8] = norm_mlp_w[l].reshape(8, 128).T
        pcol[:, l, CW:CW + 48] = ssd_conv_w[l].reshape(4, 12, 128).transpose(2, 1, 0).reshape(128, 48)
        pcol[:, l, CB:CB + 12] = ssd_conv_b[l].reshape(12, 128).T
        pcol[:, l, SNW:SNW + 8] = ssd_norm_w[l].reshape(8, 128).T
        pcol[:, l, QW] = np.tile(q_norm_w[l], 2)
        pcol[:, l, KW] = np.tile(k_norm_w[l], 2)
        pcol[:, l, DWW:DWW + 124] = cm_dw_w[l].reshape(31, 4, 128).transpose(2, 1, 0).reshape(128, 124)
        pcol[:, l, DWB:DWB + 4] = cm_dw_b[l].reshape(4, 128).T
        pcol[:, l, LNW:LNW + 4] = cm_ln_w[l].reshape(4, 128).T
        pcol[:, l, LNB:LNB + 4] = cm_ln_b[l].reshape(4, 128).T
        prow[:, l, DTB:DTB + 16] = ssd_dt_bias[l][None, :]
        prow[:, l, ALOG:ALOG + 16] = ssd_a_log[l][None, :]
        prow[:, l, SINK:SINK + 8] = attn_sinks[l][None, :]
        prow[:, l, DROW:DROW + 16] = ssd_d[l][None, :]
    return pcol, prow


_CACHE = {}


def kernel(x, norm_mix_w, w_in, ssd_conv_w, ssd_conv_b, ssd_dt_bias, ssd_a_log, ssd_d, ssd_norm_w, q_norm_w, k_norm_w,
           attn_sinks, cm_dw_w, cm_dw_b, cm_ln_w, cm_ln_b, w_out, norm_mlp_w, w_mlp_up, w_mlp_down):
    f = lambda a: np.ascontiguousarray(np.asarray(a, dtype=np.float32))
    x = f(x)
    B, L, _ = x.shape
    NL = w_in.shape[0]
    pcol, prow = host_params(NL, f(norm_mix_w), f(ssd_conv_w), f(ssd_conv_b), f(ssd_dt_bias), f(ssd_a_log), f(ssd_d), f(ssd_norm_w),
                             f(q_norm_w), f(k_norm_w), f(attn_sinks), f(cm_dw_w), f(cm_dw_b), f(cm_ln_w), f(cm_ln_b), f(norm_mlp_w))
    cst, cosT, sinT = host_consts(L)
    key = (NL, L // 512)
    if key not in _CACHE:
        _CACHE[key] = build_program(NL, L // 512)
    nc = _CACHE[key]
    shared = {"w_in": f(w_in), "w_out": f(w_out), "w_up": f(w_mlp_up), "w_down": f(w_mlp_down),
              "pcol": pcol, "prow": prow, "cst": cst, "cosT": cosT, "sinT": sinT}
    in_maps = [dict(shared, x=np.ascontiguousarray(x[b])) for b in range(B)]
    res = run_bass_kernel_spmd(nc, in_maps, core_ids=list(range(B)))
    return np.stack([np.asarray(r["out"], dtype=np.float32) for r in res.results], axis=0)
```
